# Optimizing a Trainium2 kernel written in Bass

```python
import jax, jax.numpy as jnp
from jax import lax
import numpy as np

D_MODEL = 4096
BATCH = 2
SEQ = 8192
DEPTH = 2

HEAD_DIM = 128
CONV_CH = 1024
CONV_WIDTH = 31
NSA_HEADS = 12
NSA_KV_HEADS = 4
SWA_HEADS = 12
SWA_KV_HEADS = 2
D_MIX = CONV_CH + (NSA_HEADS + SWA_HEADS) * HEAD_DIM
CMP_BLOCK = 32
CMP_STRIDE = 16
SEL_BLOCK = 64
SEL_TOPK = 16
NSA_WINDOW = 512
NSA_Q_CHUNK = 32
SWA_WINDOW = 128
SWA_BLOCK = 128
ROPE_THETA = 10000.0
D_FF = 11008
LN_EPS = 1e-5
DEEPNORM_ALPHA = (2 * DEPTH) ** 0.25
DEEPNORM_BETA = (8 * DEPTH) ** -0.25
NEG_INF = -1e30
FORCE_SCORE = 1e4
NSA_KV = NSA_KV_HEADS * HEAD_DIM
SWA_KV = SWA_KV_HEADS * HEAD_DIM
IN_SPLITS = (CONV_CH, CONV_CH, NSA_HEADS * HEAD_DIM, NSA_KV, NSA_KV, NSA_KV, NSA_KV, NSA_KV, NSA_KV, NSA_HEADS * 3, SWA_HEADS * HEAD_DIM, SWA_KV, SWA_KV)
IN_VALUE = (True, False, False, False, True, False, True, False, True, False, False, False, True)
D_IN = sum(IN_SPLITS)

kernel_name = "hybrid_conformerconv_nsa_swasink_macaron_deepnorm"


def _layernorm(x, g, b):
    xf = x.astype(jnp.float32)
    mu = jnp.mean(xf, axis=-1, keepdims=True)
    var = jnp.mean(jnp.square(xf - mu), axis=-1, keepdims=True)
    return ((xf - mu) * lax.rsqrt(var + LN_EPS) * g + b).astype(x.dtype)


def _swiglu(x, w_gu, w_down):
    gate, up = jnp.split(x @ w_gu, 2, axis=-1)
    return (jax.nn.silu(gate) * up) @ w_down


def _rope_tables(positions):
    inv = jnp.power(ROPE_THETA, -jnp.arange(0, HEAD_DIM, 2, dtype=jnp.float32) / HEAD_DIM)
    ang = positions.astype(jnp.float32)[..., None] * inv
    return jnp.cos(ang)[:, :, None, :], jnp.sin(ang)[:, :, None, :]


def _rope(x, cos, sin):
    x1, x2 = jnp.split(x.astype(jnp.float32), 2, axis=-1)
    return jnp.concatenate([x1 * cos - x2 * sin, x2 * cos + x1 * sin], axis=-1).astype(x.dtype)


def _conv_module(val, gate, dw_w, dw_b, ln_g, ln_b):
    u = val * jax.nn.sigmoid(gate)
    y = lax.conv_general_dilated(u, dw_w[:, None, :], window_strides=(1,), padding=[(CONV_WIDTH - 1, 0)],
                                 dimension_numbers=('NWC', 'WIO', 'NWC'), feature_group_count=CONV_CH) + dw_b
    return jax.nn.silu(_layernorm(y, ln_g, ln_b))


def _compress(t, pos_emb, w1, w2):
    B, S, H, Dh = t.shape
    n_cmp = (S - CMP_BLOCK) // CMP_STRIDE + 1
    idx = np.arange(n_cmp)[:, None] * CMP_STRIDE + np.arange(CMP_BLOCK)[None, :]
    blocks = t[:, idx] + pos_emb[:, None, :]
    blocks = jnp.moveaxis(blocks, 2, 3).reshape(B, n_cmp, H, CMP_BLOCK * Dh)
    return jax.nn.silu(blocks @ w1) @ w2


def _cmp_to_sel(n_cmp, n_sel):
    cs = np.arange(n_cmp) * CMP_STRIDE
    ss = np.arange(n_sel) * SEL_BLOCK
    return ((cs[:, None] < ss[None, :] + SEL_BLOCK) & (cs[:, None] + CMP_BLOCK > ss[None, :])).astype(np.float32)


def _nsa(q, k_cmp, v_cmp, k_slc, v_slc, k_win, v_win, gates):
    B, S = q.shape[:2]
    G = NSA_HEADS // NSA_KV_HEADS
    scale = HEAD_DIM ** -0.5
    n_cmp = k_cmp.shape[1]
    n_sel = S // SEL_BLOCK
    top_k = min(SEL_TOPK, n_sel)
    cmp_end = jnp.arange(n_cmp) * CMP_STRIDE + (CMP_BLOCK - 1)
    overlap = jnp.asarray(_cmp_to_sel(n_cmp, n_sel))
    kbt = k_slc.reshape(B, n_sel, SEL_BLOCK, NSA_KV_HEADS, HEAD_DIM).transpose(0, 3, 1, 2, 4)
    vbt = v_slc.reshape(B, n_sel, SEL_BLOCK, NSA_KV_HEADS, HEAD_DIM).transpose(0, 3, 1, 2, 4)
    pad = ((0, 0), (NSA_WINDOW, 0), (0, 0), (0, 0))
    kw_pad = jnp.pad(k_win, pad)
    vw_pad = jnp.pad(v_win, pad)
    qg = q.reshape(B, S, NSA_KV_HEADS, G, HEAD_DIM)
    gg = gates.reshape(B, S, NSA_KV_HEADS, G, 3)
    b_ix = jnp.arange(B)[:, None, None, None]
    h_ix = jnp.arange(NSA_KV_HEADS)[None, None, :, None]
    blk = jnp.arange(n_sel)

    def chunk(c):
        start = c * NSA_Q_CHUNK
        qc = lax.dynamic_slice_in_dim(qg, start, NSA_Q_CHUNK, axis=1)
        gc = lax.dynamic_slice_in_dim(gg, start, NSA_Q_CHUNK, axis=1)
        t_pos = start + jnp.arange(NSA_Q_CHUNK)
        s_c = jnp.einsum('bthgd,bnhd->bthgn', qc, k_cmp, preferred_element_type=jnp.float32) * scale
        m_c = (cmp_end[None, :] <= t_pos[:, None])[None, :, None, None, :]
        p_c = jnp.where(m_c, jax.nn.softmax(jnp.where(m_c, s_c, NEG_INF), axis=-1), 0.0)
        o_c = jnp.einsum('bthgn,bnhd->bthgd', p_c.astype(v_cmp.dtype), v_cmp)
        imp = jnp.einsum('bthgn,nj->bthj', p_c, overlap)
        t_blk = t_pos // SEL_BLOCK
        forced = (blk[None, :] == 0) | (blk[None, :] == t_blk[:, None]) | (blk[None, :] == t_blk[:, None] - 1)
        causal_blk = blk[None, :] <= t_blk[:, None]
        imp = jnp.where(forced[None, :, None, :], FORCE_SCORE, imp)
        imp = jnp.where(causal_blk[None, :, None, :], imp, NEG_INF)
        _, top_idx = lax.top_k(imp, top_k)
        sel_ok = top_idx <= t_blk[None, :, None, None]
        ks = kbt[b_ix, h_ix, top_idx]
        vs = vbt[b_ix, h_ix, top_idx]
        s_s = jnp.einsum('bthgd,bthkjd->bthgkj', qc, ks, preferred_element_type=jnp.float32) * scale
        key_pos = top_idx[..., None] * SEL_BLOCK + jnp.arange(SEL_BLOCK)
        m_s = (sel_ok[..., None] & (key_pos <= t_pos[None, :, None, None, None]))
        m_s = m_s.reshape(B, NSA_Q_CHUNK, NSA_KV_HEADS, 1, top_k * SEL_BLOCK)
        s_s = s_s.reshape(B, NSA_Q_CHUNK, NSA_KV_HEADS, G, top_k * SEL_BLOCK)
        p_s = jax.nn.softmax(jnp.where(m_s, s_s, NEG_INF), axis=-1).reshape(B, NSA_Q_CHUNK, NSA_KV_HEADS, G, top_k, SEL_BLOCK)
        o_s = jnp.einsum('bthgkj,bthkjd->bthgd', p_s.astype(vs.dtype), vs)
        kwc = lax.dynamic_slice_in_dim(kw_pad, start, NSA_WINDOW + NSA_Q_CHUNK, axis=1)
        vwc = lax.dynamic_slice_in_dim(vw_pad, start, NSA_WINDOW + NSA_Q_CHUNK, axis=1)
        kpos = start - NSA_WINDOW + jnp.arange(NSA_WINDOW + NSA_Q_CHUNK)
        m_w = (kpos[None, :] <= t_pos[:, None]) & (kpos[None, :] > t_pos[:, None] - NSA_WINDOW) & (kpos[None, :] >= 0)
        s_w = jnp.einsum('bthgd,bjhd->bthgj', qc, kwc, preferred_element_type=jnp.float32) * scale
        p_w = jax.nn.softmax(jnp.where(m_w[None, :, None, None, :], s_w, NEG_INF), axis=-1)
        o_w = jnp.einsum('bthgj,bjhd->bthgd', p_w.astype(vwc.dtype), vwc)
        o = gc[..., 0:1] * o_c + gc[..., 1:2] * o_s + gc[..., 2:3] * o_w
        return o.reshape(B, NSA_Q_CHUNK, NSA_HEADS * HEAD_DIM)

    out = lax.map(chunk, jnp.arange(S // NSA_Q_CHUNK))
    return jnp.moveaxis(out, 0, 1).reshape(B, S, NSA_HEADS * HEAD_DIM)


def _swa_sinks(q, k, v, sinks):
    B, S = q.shape[:2]
    nb = S // SWA_BLOCK
    G = SWA_HEADS // SWA_KV_HEADS
    qb = q.reshape(B, nb, SWA_BLOCK, SWA_KV_HEADS, G, HEAD_DIM)

    def band(t):
        tb = t.reshape(B, nb, SWA_BLOCK, SWA_KV_HEADS, HEAD_DIM)
        prev = jnp.pad(tb, ((0, 0), (1, 0), (0, 0), (0, 0), (0, 0)))[:, :-1]
        return jnp.concatenate([prev, tb], axis=2)

    kb, vb = band(k), band(v)
    s = jnp.einsum('bnqhgd,bnjhd->bnhgqj', qb, kb, preferred_element_type=jnp.float32) * HEAD_DIM ** -0.5
    qi = jnp.arange(SWA_BLOCK)[:, None]
    ji = jnp.arange(2 * SWA_BLOCK)[None, :]
    diff = qi + SWA_BLOCK - ji
    band_ok = (diff >= 0) & (diff < SWA_WINDOW)
    first_ok = ji >= SWA_BLOCK
    blk_first = (jnp.arange(nb) == 0)[:, None, None]
    mask = band_ok[None] & (~blk_first | first_ok[None])
    s = jnp.where(mask[None, :, None, None], s, NEG_INF)
    sink = jnp.broadcast_to(sinks.astype(jnp.float32).reshape(1, 1, SWA_KV_HEADS, G, 1, 1), s.shape[:-1] + (1,))
    p = jax.nn.softmax(jnp.concatenate([s, sink], axis=-1), axis=-1)[..., :-1]
    o = jnp.einsum('bnhgqj,bnjhd->bnqhgd', p.astype(vb.dtype), vb)
    return o.reshape(B, S, SWA_HEADS * HEAD_DIM)


def _mixer(x, cos, sin, w_in, w_out, dw_w, dw_b, cln_g, cln_b, cmp_pos, cmp_w1, cmp_w2, gate_b, sinks):
    B, S, _ = x.shape
    offsets = np.cumsum(IN_SPLITS)[:-1].tolist()
    (a_val, a_gate, q_b, kc, vc, ksl, vsl, kw, vw, g_b, q_c, k_c, v_c) = jnp.split(x @ w_in, offsets, axis=-1)

    def heads(t, n):
        return t.reshape(B, S, n, HEAD_DIM)

    out_a = _conv_module(a_val, a_gate, dw_w, dw_b, cln_g, cln_b)
    qn = _rope(heads(q_b, NSA_HEADS), cos, sin)
    k_cmp = _compress(_rope(heads(kc, NSA_KV_HEADS), cos, sin), cmp_pos[0], cmp_w1[0], cmp_w2[0])
    v_cmp = _compress(heads(vc, NSA_KV_HEADS), cmp_pos[1], cmp_w1[1], cmp_w2[1])
    gates = jax.nn.sigmoid(g_b + gate_b).reshape(B, S, NSA_HEADS, 3)
    out_b = _nsa(qn, k_cmp, v_cmp, _rope(heads(ksl, NSA_KV_HEADS), cos, sin), heads(vsl, NSA_KV_HEADS),
                 _rope(heads(kw, NSA_KV_HEADS), cos, sin), heads(vw, NSA_KV_HEADS), gates)
    out_c = _swa_sinks(_rope(heads(q_c, SWA_HEADS), cos, sin), _rope(heads(k_c, SWA_KV_HEADS), cos, sin),
                       heads(v_c, SWA_KV_HEADS), sinks)
    return jnp.concatenate([out_a, out_b, out_c], axis=-1) @ w_out


def setup_inputs(seed: int = 0) -> dict:
    key = jax.random.key(seed)
    ks = jax.random.split(key, 24)
    L = DEPTH

    def nrm(k, shape, scale):
        return jax.random.normal(k, shape, jnp.float32) * scale

    col_scale = jnp.asarray(np.concatenate([np.full(n, DEEPNORM_BETA if v else 1.0, np.float32) for n, v in zip(IN_SPLITS, IN_VALUE)]))
    return {
        'x': nrm(ks[0], (BATCH, SEQ, D_MODEL), 1.0),
        'positions': jnp.broadcast_to(jnp.arange(SEQ, dtype=jnp.int32), (BATCH, SEQ)),
        'ffn1_w_gu': nrm(ks[1], (L, D_MODEL, 2 * D_FF), D_MODEL ** -0.5),
        'ffn1_w_down': nrm(ks[2], (L, D_FF, D_MODEL), D_FF ** -0.5 * DEEPNORM_BETA),
        'ln1_g': 1.0 + nrm(ks[3], (L, D_MODEL), 0.02),
        'ln1_b': nrm(ks[4], (L, D_MODEL), 0.02),
        'w_in': nrm(ks[5], (L, D_MODEL, D_IN), D_MODEL ** -0.5) * col_scale,
        'conv_dw_w': nrm(ks[6], (L, CONV_WIDTH, CONV_CH), CONV_WIDTH ** -0.5),
        'conv_dw_b': nrm(ks[7], (L, CONV_CH), 0.01),
        'conv_ln_g': 1.0 + nrm(ks[8], (L, CONV_CH), 0.02),
        'conv_ln_b': nrm(ks[9], (L, CONV_CH), 0.02),
        'nsa_cmp_pos': nrm(ks[10], (L, 2, CMP_BLOCK, HEAD_DIM), 0.02),
        'nsa_cmp_w1': nrm(ks[11], (L, 2, CMP_BLOCK * HEAD_DIM, HEAD_DIM), (CMP_BLOCK * HEAD_DIM) ** -0.5),
        'nsa_cmp_w2': nrm(ks[12], (L, 2, HEAD_DIM, HEAD_DIM), HEAD_DIM ** -0.5),
        'nsa_gate_b': nrm(ks[13], (L, NSA_HEADS * 3), 0.1),
        'swa_sinks': nrm(ks[14], (L, SWA_HEADS), 0.5),
        'w_out': nrm(ks[15], (L, D_MIX, D_MODEL), D_MIX ** -0.5 * DEEPNORM_BETA),
        'ln2_g': 1.0 + nrm(ks[16], (L, D_MODEL), 0.02),
        'ln2_b': nrm(ks[17], (L, D_MODEL), 0.02),
        'ffn2_w_gu': nrm(ks[18], (L, D_MODEL, 2 * D_FF), D_MODEL ** -0.5),
        'ffn2_w_down': nrm(ks[19], (L, D_FF, D_MODEL), D_FF ** -0.5 * DEEPNORM_BETA),
        'ln3_g': 1.0 + nrm(ks[20], (L, D_MODEL), 0.02),
        'ln3_b': nrm(ks[21], (L, D_MODEL), 0.02),
    }


def reference(x, positions, ffn1_w_gu, ffn1_w_down, ln1_g, ln1_b, w_in, conv_dw_w, conv_dw_b, conv_ln_g, conv_ln_b,
              nsa_cmp_pos, nsa_cmp_w1, nsa_cmp_w2, nsa_gate_b, swa_sinks, w_out, ln2_g, ln2_b,
              ffn2_w_gu, ffn2_w_down, ln3_g, ln3_b):
    cos, sin = _rope_tables(positions)
    for i in range(DEPTH):
        x = _layernorm(DEEPNORM_ALPHA * x + 0.5 * _swiglu(x, ffn1_w_gu[i], ffn1_w_down[i]), ln1_g[i], ln1_b[i])
        mix = _mixer(x, cos, sin, w_in[i], w_out[i], conv_dw_w[i], conv_dw_b[i], conv_ln_g[i], conv_ln_b[i],
                     nsa_cmp_pos[i], nsa_cmp_w1[i], nsa_cmp_w2[i], nsa_gate_b[i], swa_sinks[i])
        x = _layernorm(DEEPNORM_ALPHA * x + mix, ln2_g[i], ln2_b[i])
        x = _layernorm(DEEPNORM_ALPHA * x + 0.5 * _swiglu(x, ffn2_w_gu[i], ffn2_w_down[i]), ln3_g[i], ln3_b[i])
    return x
```

```python
import numpy as np
import concourse.bass as bass
import concourse.mybir as mybir
from contextlib import ExitStack

F32 = mybir.dt.float32
BF16 = mybir.dt.bfloat16
I32 = mybir.dt.int32
AF = mybir.ActivationFunctionType
ALU = mybir.AluOpType


class Tok:
    __slots__ = ("w", "r")

    def __init__(self):
        self.w = {}
        self.r = {}


class Sched:
    def __init__(self, nc, es):
        self.nc = nc
        self.es = es
        self.q = {e: [] for e in ("pe", "act", "dve", "pool", "sp")}
        self.sem = {}
        self.cnt = {}
        self.seen = {e: {} for e in self.q}
        self.nops = 0

    def _sem(self, name):
        if name not in self.sem:
            self.sem[name] = self.es.enter_context(self.nc.semaphore(name))
            self.cnt[name] = 0
        return name

    def op(self, e, fn, reads=(), writes=(), inc=True, dma=None, pwrites=()):
        deps = {}

        def need(ev):
            s, v = ev
            if deps.get(s, 0) < v:
                deps[s] = v

        for t in reads:
            for ev in t.w.items():
                need(ev)
        for t in writes:
            for ev in t.w.items():
                need(ev)
            for ev in t.r.items():
                need(ev)
        for t in pwrites:
            for ev in t.r.items():
                need(ev)
        own = "c_" + e
        for s, v in deps.items():
            if dma is None and s == own and e == "pe":
                continue
            if self.seen[e].get(s, 0) >= v:
                continue
            self.seen[e][s] = v
            self.q[e].append(("w", s, v))
        if dma is not None:
            s = self._sem("d_" + dma)
            self.cnt[s] += 16
            ev = (s, self.cnt[s])
            self.q[e].append(("o", fn, s, 16))
        else:
            s = self._sem(own)
            if inc:
                self.cnt[s] += 1
                ev = (s, self.cnt[s])
                self.q[e].append(("o", fn, s, 1))
            else:
                assert e == "pe"
                ev = (s, self.cnt[s] + 1)
                self.q[e].append(("o", fn, None, 0))
        for t in reads:
            if t.r.get(ev[0], 0) < ev[1]:
                t.r[ev[0]] = ev[1]
        for t in writes:
            t.w = {ev[0]: ev[1]}
            t.r = {}
        for t in pwrites:
            if t.w.get(ev[0], 0) < ev[1]:
                t.w[ev[0]] = ev[1]
            t.r = {}
        self.nops += 1

    def finish(self):
        for s, v in self.cnt.items():
            if v > 0 and self.seen["sp"].get(s, 0) < v:
                self.q["sp"].append(("w", s, v))

    def emit(self):
        nc = self.nc
        self.finish()
        with nc.Block() as block:

            def rep(e):
                def f(eng):
                    for it in self.q[e]:
                        if it[0] == "w":
                            eng.wait_ge(self.sem[it[1]], it[2])
                        else:
                            ins = it[1](eng)
                            if it[2] is not None:
                                ins.then_inc(self.sem[it[2]], it[3])

                return f

            block.tensor(rep("pe"))
            block.scalar(rep("act"))
            block.vector(rep("dve"))
            block.gpsimd(rep("pool"))
            block.sync(rep("sp"))


class Ctx:
    def __init__(self, nc, es):
        self.nc = nc
        self.es = es
        self.S = Sched(nc, es)
        self.n = 0

    def sb(self, name, shape, dt):
        self.n += 1
        return self.es.enter_context(self.nc.sbuf_tensor(f"{name}_{self.n}", shape, dt))

    def ps(self, name, shape, dt):
        self.n += 1
        return self.es.enter_context(self.nc.psum_tensor(f"{name}_{self.n}", shape, dt))


def mm(S, out, lhsT, rhs, start, stop, reads, writes, inc, **kw):
    S.op("pe", lambda e: e.matmul(out, lhsT, rhs, start=start, stop=stop, **kw), reads=reads, writes=writes, inc=inc)


def V(S, eng, method, *args, reads=(), writes=(), pwrites=(), **kw):
    S.op(eng, lambda e: getattr(e, method)(*args, **kw), reads=reads, writes=writes, pwrites=pwrites)


def DMA(S, q, out, in_, chan, reads=(), writes=(), pwrites=()):
    S.op(q, lambda e: e.dma_start(out=out, in_=in_), reads=reads, writes=writes, pwrites=pwrites, dma=chan)


def barrier(S):
    snap = dict(S.cnt)
    for e in S.q:
        for s, v in snap.items():
            if v > 0 and S.seen[e].get(s, 0) < v:
                S.seen[e][s] = v
                S.q[e].append(("w", s, v))


def build_xT(C, x_dram, r0, XT, t_XT, col0, ident, xrow, t_xrow, pst, t_pst, D, gcol=None, bcol=None, cnt=[0]):
    S = C.S
    DC = D // 128
    S.op("sp", lambda e: e.dma_start(out=xrow[:, 0:D], in_=x_dram[r0:r0 + 128, :]), writes=[t_xrow], dma="xrow")
    for g0 in range(0, DC, 4):
        k = cnt[0] % 2
        cnt[0] += 1
        n = min(4, DC - g0)
        for j in range(n):
            dc = g0 + j
            S.op("pe", (lambda dc=dc, j=j, k=k: lambda e: e.transpose(pst[k][:, j * 128:(j + 1) * 128], xrow[:, dc * 128:(dc + 1) * 128], ident[:]))(),
                 reads=[t_xrow], writes=[t_pst[k]], inc=(j == n - 1))
        src = pst[k][:, 0:n * 128].rearrange("p (a b) -> p a b", b=128)
        dst = XT[:, g0:g0 + n, col0:col0 + 128]
        if k == 0:
            S.op("dve", (lambda dst=dst, src=src: lambda e: e.tensor_copy(dst, src))(), reads=[t_pst[k]], writes=[t_XT])
        else:
            S.op("act", (lambda dst=dst, src=src: lambda e: e.copy(dst, src))(), reads=[t_pst[k]], writes=[t_XT])


def ln_rows(C, z_dram, zr0, t_z, y_dram, yr0, t_y, g_d, b_d, xrow, t_xrow, D, eps, tmp):
    S = C.S
    stats, mv, sd, rstd, gb, t_gb = tmp["stats"], tmp["mv"], tmp["sd"], tmp["rstd"], tmp["gb"], tmp["t_gb"]
    t_small = tmp["t_small"]
    NCH = D // 512 if D >= 512 else 1
    CW = min(512, D)
    S.op("sp", lambda e: e.dma_start(out=xrow[:, 0:D], in_=z_dram[zr0:zr0 + 128, :]), reads=[t_z], writes=[t_xrow], dma="xrow")
    for k in range(NCH):
        S.op("dve", (lambda k=k: lambda e: e.bn_stats(stats[:, k, :], xrow[:, k * CW:(k + 1) * CW]))(), reads=[t_xrow], writes=[t_small])
    S.op("dve", lambda e: e.bn_aggr(mv[:, :], stats[:, 0:NCH, :]), reads=[t_small], writes=[t_small])
    S.op("act", lambda e: e.activation(sd[:, :], mv[:, 1:2], AF.Sqrt, bias=tmp["eps"][:, 0:1], scale=1.0), reads=[t_small], writes=[t_small])
    S.op("dve", lambda e: e.reciprocal(rstd[:, :], sd[:, :]), reads=[t_small], writes=[t_small])
    S.op("dve", lambda e: e.tensor_scalar(xrow[:, 0:D], xrow[:, 0:D], mv[:, 0:1], rstd[:, 0:1], ALU.subtract, ALU.mult), reads=[t_small, t_xrow], writes=[t_xrow])
    for k in range(NCH):
        i = tmp["gbi"][0] % 2
        tmp["gbi"][0] += 1
        S.op("sp", (lambda k=k, i=i: lambda e: e.dma_start(out=gb[i][:, 0, 0:CW], in_=g_d[k * CW:(k + 1) * CW].partition_broadcast(128)))(), writes=[t_gb[i]], dma=f"gb{i}")
        S.op("sp", (lambda k=k, i=i: lambda e: e.dma_start(out=gb[i][:, 1, 0:CW], in_=b_d[k * CW:(k + 1) * CW].partition_broadcast(128)))(), pwrites=[t_gb[i]], dma=f"gb{i}")
        sl = xrow[:, k * CW:(k + 1) * CW]
        S.op("pool", (lambda sl=sl, i=i: lambda e: e.tensor_tensor(sl, sl, gb[i][:, 0, 0:CW], ALU.mult))(), reads=[t_gb[i], t_xrow], writes=[t_xrow])
        S.op("pool", (lambda sl=sl, i=i: lambda e: e.tensor_tensor(sl, sl, gb[i][:, 1, 0:CW], ALU.add))(), reads=[t_gb[i], t_xrow], writes=[t_xrow])
    S.op("sp", lambda e: e.dma_start(out=y_dram[yr0:yr0 + 128, :], in_=xrow[:, 0:D]), reads=[t_xrow], pwrites=[t_y], dma="xrow_st")


def alloc_common(C, D):
    nc = C.nc
    cm = {}
    cm["ident"] = C.sb("ident", [128, 128], F32)
    cm["identb"] = C.sb("identb", [128, 128], BF16)
    cm["t_const"] = Tok()
    cm["xrow"] = C.sb("xrow", [128, D], F32)
    cm["t_xrow"] = Tok()
    cm["ps"] = [C.ps(f"ps{i}", [128, 512], F32) for i in range(8)]
    cm["t_ps"] = [Tok() for _ in range(8)]
    tmp = {
        "stats": C.sb("stats", [128, 8, 6], F32), "mv": C.sb("mv", [128, 2], F32), "sd": C.sb("sd", [128, 1], F32),
        "rstd": C.sb("rstd", [128, 1], F32), "gb": [C.sb(f"gb{i}", [128, 2, 512], F32) for i in range(2)],
        "t_gb": [Tok(), Tok()], "t_small": Tok(), "gbi": [0], "eps": C.sb("epsc", [128, 1], F32),
    }
    cm["tmp"] = tmp
    return cm


def init_common(C, cm, ident_d):
    S = C.S
    S.op("sp", lambda e: e.dma_start(out=cm["ident"][:, :], in_=ident_d[:, :]), writes=[cm["t_const"]], dma="const")
    S.op("dve", lambda e: e.tensor_copy(cm["identb"][:, :], cm["ident"][:, :]), reads=[cm["t_const"]], writes=[cm["t_const"]])
    S.op("dve", lambda e: e.memset(cm["tmp"]["eps"][:, :], 1e-5), writes=[cm["tmp"]["t_small"]])


def down_ln(C, cm, HT, t_HT, FC, wd_v, Wb, t_W, load_w, wi, xi, x_dram, t_x, z_dram, t_zrows, y_dram, t_y, g_d, b_d, xres, t_xres, zt, t_zt, r0, T, D, alpha, eps):
    S = C.S
    ps, t_ps = cm["ps"], cm["t_ps"]
    xrow, t_xrow = cm["xrow"], cm["t_xrow"]
    TS = T // 128
    OW = min(512, D)
    OT = D // OW
    FG = 8
    NG = (FC + FG - 1) // FG
    for ot in range(OT):
        for fg in range(NG):
            idx = wi[0] % 2
            wi[0] += 1
            nf = min(FG, FC - fg * FG)
            Wv = Wb[idx][:, 0:FG * OW].rearrange("p (c n) -> p c n", n=OW)
            load_w(Wv[:, 0:nf, :], wd_v[:, fg * FG:fg * FG + nf, ot * OW:(ot + 1) * OW], idx)
            for ts in range(TS):
                for j in range(nf):
                    fc = fg * FG + j
                    mm(S, ps[ts][:, 0:OW], HT[:, fc, ts * 128:(ts + 1) * 128], Wv[:, j, :], fc == 0, fc == FC - 1,
                       reads=[t_W[idx], t_HT], writes=[t_ps[ts]], inc=(fc == FC - 1 or j == nf - 1))
        for ts in range(TS):
            i = xi[0] % 2
            xi[0] += 1
            rr = r0 + ts * 128
            S.op("sp", (lambda i=i, rr=rr, ot=ot: lambda e: e.dma_start(out=xres[i][:, :], in_=x_dram[rr:rr + 128, ot * OW:(ot + 1) * OW]))(),
                 reads=[t_x], writes=[t_xres[i]], dma=f"xres{i}")
            S.op("dve", (lambda i=i, ts=ts: lambda e: e.scalar_tensor_tensor(zt[i][:, :], xres[i][:, :], float(alpha), ps[ts][:, 0:OW], ALU.mult, ALU.add))(),
                 reads=[t_xres[i], t_ps[ts]], writes=[t_zt[i]])
            S.op("sp", (lambda i=i, rr=rr, ot=ot: lambda e: e.dma_start(out=z_dram[rr:rr + 128, ot * OW:(ot + 1) * OW], in_=zt[i][:, :]))(),
                 reads=[t_zt[i]], pwrites=[t_zrows[rr // 128]], dma=f"zt{i}")
    for ts in range(TS):
        rr = r0 + ts * 128
        ln_rows(C, z_dram, rr, t_zrows[rr // 128], y_dram, rr, t_y, g_d, b_d, xrow, t_xrow, D, eps, cm["tmp"])


def ffn_stage(C, cm, x_dram, wgu, wd, g_d, b_d, z_dram, y_dram, NT, D, F, alpha, eps, t_x=None, t_z=None, t_y=None, cast_dma=True):
    S = C.S
    DC, FC = D // 128, F // 128
    T = min(512, NT)
    NTT = NT // T
    TS = T // 128
    OW = min(512, D)
    OT = D // OW
    FG = 8
    ident = cm["ident"]
    xrow, t_xrow = cm["xrow"], cm["t_xrow"]
    ps, t_ps = cm["ps"], cm["t_ps"]
    XT = C.sb("XT", [128, DC, T], BF16)
    t_XT = Tok()
    HT = C.sb("HT", [128, FC, T], BF16)
    t_HT = Tok()
    WSZ = max(DC * 256, FG * OW)
    Wb = [C.sb(f"W{i}", [128, WSZ], BF16) for i in range(2)]
    t_W = [Tok(), Tok()]
    if not cast_dma:
        Wst = [C.sb(f"Wst{i}", [128, WSZ], F32) for i in range(2)]
        t_Wst = [Tok(), Tok()]
    sg = [C.sb(f"sg{i}", [128, T], F32) for i in range(4)]
    t_sg = [Tok() for _ in range(4)]
    xres = [C.sb(f"xres{i}", [128, OW], F32) for i in range(2)]
    t_xres = [Tok(), Tok()]
    zt = [C.sb(f"zt{i}", [128, OW], F32) for i in range(2)]
    t_zt = [Tok(), Tok()]
    t_x = t_x or Tok()
    t_zrows = [Tok() for _ in range(NT // 128)]
    t_y = t_y or Tok()
    wgu_v = wgu.rearrange("(c p) n -> p c n", p=128)
    wd_v = wd.rearrange("(c p) n -> p c n", p=128)
    wi = [0]
    xi = [0]

    def load_w(dst3, src3, idx):
        if cast_dma:
            S.op("pool", lambda e: e.dma_start(out=dst3, in_=src3), writes=[t_W[idx]], dma=f"w{idx}")
        else:
            a, b = dst3.shape[1], dst3.shape[2]
            st = Wst[idx][:, 0:a * b].rearrange("p (a b) -> p a b", b=b)
            S.op("sp", lambda e: e.dma_start(out=st, in_=src3), writes=[t_Wst[idx]], dma=f"w{idx}")
            S.op("pool", lambda e: e.tensor_copy(dst3, st), reads=[t_Wst[idx]], writes=[t_W[idx]])

    for tt in range(NTT):
        r0 = tt * T
        for ts in range(TS):
            build_xT(C, x_dram, r0 + ts * 128, XT, t_XT, ts * 128, ident, xrow, t_xrow, ps[4:6], t_ps[4:6], D)
        assert FC % 2 == 0
        for fp in range(FC // 2):
            for kind in range(2):
                idx = wi[0] % 2
                wi[0] += 1
                Wv = Wb[idx][:, 0:DC * 256].rearrange("p (c n) -> p c n", n=256)
                load_w(Wv, wgu_v[:, :, kind * F + fp * 256:kind * F + (fp + 1) * 256], idx)
                for h in range(2):
                    fc = 2 * fp + h
                    bank = kind * 2 + h
                    for dc in range(DC):
                        mm(S, ps[bank][:, 0:T], Wv[:, dc, h * 128:(h + 1) * 128], XT[:, dc, :], dc == 0, dc == DC - 1,
                           reads=[t_W[idx], t_XT], writes=[t_ps[bank]], inc=(dc == DC - 1))
                    si = (fp % 2) * 2 + h
                    if kind == 0:
                        V(S, "act", "activation", sg[si][:, 0:T], ps[bank][:, 0:T], AF.Silu, reads=[t_ps[bank]], writes=[t_sg[si]])
                    else:
                        V(S, "dve", "scalar_tensor_tensor", HT[:, fc, :], sg[si][:, 0:T], 0.5, ps[bank][:, 0:T], ALU.mult, ALU.mult,
                          reads=[t_sg[si], t_ps[bank]], pwrites=[t_HT] if fc > 0 else (), writes=[t_HT] if fc == 0 else ())
        down_ln(C, cm, HT, t_HT, FC, wd_v, Wb, t_W, load_w, wi, xi, x_dram, t_x, z_dram, t_zrows, y_dram, t_y, g_d, b_d, xres, t_xres, zt, t_zt, r0, T, D, alpha, eps)
    return t_y


def sub_ctx(C, es):
    C2 = Ctx.__new__(Ctx)
    C2.nc, C2.es, C2.S, C2.n = C.nc, es, C.S, C.n + 1000
    return C2


def outproj_stage(C, cm, mixT_d, x_dram, wo, g_d, b_d, z_dram, y_dram, NT, D, alpha, eps):
    S = C.S
    DC = D // 128
    T = min(512, NT)
    OW = min(512, D)
    MT = [C.sb(f"MT{i}", [128, DC, T], BF16) for i in range(2)]
    t_MT = [Tok(), Tok()]
    Wb = [C.sb(f"Wo{i}", [128, 8 * OW], BF16) for i in range(2)]
    t_W = [Tok(), Tok()]
    xres = [C.sb(f"xres{i}", [128, OW], F32) for i in range(2)]
    t_xres = [Tok(), Tok()]
    zt = [C.sb(f"zt{i}", [128, OW], F32) for i in range(2)]
    t_zt = [Tok(), Tok()]
    t_zrows = [Tok() for _ in range(NT // 128)]
    t_y = Tok()
    t_x = Tok()
    wo_v = wo.rearrange("(c p) n -> p c n", p=128)
    mT_v = mixT_d.rearrange("(c p) t -> p c t", p=128)
    wi = [0]
    xi = [0]

    def load_w(dst3, src3, idx):
        S.op("pool", lambda e: e.dma_start(out=dst3, in_=src3), writes=[t_W[idx]], dma=f"wo{idx}")

    for tt in range(NT // T):
        k = tt % 2
        r0 = tt * T
        DMA(S, "sp", MT[k][:, :, :], mT_v[:, :, r0:r0 + T], f"mt{k}", writes=[t_MT[k]])
        down_ln(C, cm, MT[k], t_MT[k], DC, wo_v, Wb, t_W, load_w, wi, xi, x_dram, t_x, z_dram, t_zrows, y_dram, t_y, g_d, b_d, xres, t_xres, zt, t_zt, r0, T, D, alpha, eps)
    return t_y

import math

PI = math.pi
TWO_PI = 2 * math.pi
C1 = 6.28125
C2 = TWO_PI - C1
SCALE = 128 ** -0.5
NEGM = -30000.0
DBG = {"level": 9, "nq": None}


def xT_stage(C, cm, x_dram, ntok, xT_d, t_xT_d, D, XTt, t_XTt):
    S = C.S
    r = 0
    k = 0
    while r < ntok:
        T = min(512, ntok - r)
        for ts in range(T // 128):
            build_xT(C, x_dram, r + ts * 128, XTt[k], t_XTt[k], ts * 128, cm["ident"], cm["xrow"], cm["t_xrow"], cm["ps"][4:6], cm["t_ps"][4:6], D)
        DMA(S, "sp", xT_d[:, :, r:r + T], XTt[k][:, :, 0:T], f"xtst{k}", reads=[t_XTt[k]], pwrites=[t_xT_d])
        k ^= 1
        r += T


def alloc_rope(C):
    rt = {}
    for n in ("ang", "tmpf", "r", "r2", "m", "St", "Ct"):
        rt[n] = C.sb("rt_" + n, [128, 512], F32)
    rt["posi"] = C.sb("rt_posi", [128, 512], I32)
    rt["ki"] = C.sb("rt_ki", [128, 512], I32)
    rt["t"] = Tok()
    rt["t_tab"] = Tok()
    return rt


def rope_tables(C, cm, rt, pos_d, t0, T):
    S = C.S
    inv, sgn = cm["inv"], cm["sgn"]
    t, tt = rt["t"], rt["t_tab"]
    ang, tmpf, r, r2, m, St, Ct, posi, ki = (rt[n][:, 0:T] for n in ("ang", "tmpf", "r", "r2", "m", "St", "Ct", "posi", "ki"))
    DMA(S, "sp", posi, pos_d[t0:t0 + T].partition_broadcast(128), "posi", writes=[t])
    V(S, "dve", "tensor_copy", ang, posi, reads=[t], writes=[t])
    V(S, "dve", "tensor_scalar", ang, ang, inv[:, 0:1], None, ALU.mult, reads=[t, cm["t_const"]], writes=[t])
    V(S, "dve", "tensor_scalar", tmpf, ang, 1.0 / TWO_PI, None, ALU.mult, reads=[t], writes=[t])
    V(S, "dve", "tensor_copy", ki, tmpf, reads=[t], writes=[t])
    V(S, "dve", "tensor_copy", tmpf, ki, reads=[t], writes=[t])
    V(S, "dve", "scalar_tensor_tensor", r, tmpf, -C1, ang, ALU.mult, ALU.add, reads=[t], writes=[t])
    V(S, "dve", "scalar_tensor_tensor", r, tmpf, -C2, r, ALU.mult, ALU.add, reads=[t], writes=[t])
    V(S, "dve", "tensor_scalar", m, r, PI, None, ALU.is_gt, reads=[t], writes=[t])
    V(S, "dve", "scalar_tensor_tensor", r, m, -TWO_PI, r, ALU.mult, ALU.add, reads=[t], writes=[t])
    V(S, "dve", "tensor_scalar", m, r, -PI, None, ALU.is_lt, reads=[t], writes=[t])
    V(S, "dve", "scalar_tensor_tensor", r, m, TWO_PI, r, ALU.mult, ALU.add, reads=[t], writes=[t])
    V(S, "dve", "tensor_scalar", r2, r, PI / 2, None, ALU.add, reads=[t], writes=[t])
    V(S, "dve", "tensor_scalar", m, r2, PI, None, ALU.is_gt, reads=[t], writes=[t])
    V(S, "dve", "scalar_tensor_tensor", r2, m, -TWO_PI, r2, ALU.mult, ALU.add, reads=[t], writes=[t])
    V(S, "dve", "tensor_scalar", r, r, 3.14159, -3.14159, ALU.min, ALU.max, reads=[t], writes=[t])
    V(S, "dve", "tensor_scalar", r2, r2, 3.14159, -3.14159, ALU.min, ALU.max, reads=[t], writes=[t])
    V(S, "act", "activation", St, r, AF.Sin, scale=sgn[:, 0:1], reads=[t, cm["t_const"]], writes=[tt])
    V(S, "act", "activation", Ct, r2, AF.Sin, reads=[t], pwrites=[tt])


class Proj:
    def __init__(self, C, cm, D, XTt, t_XTt, Wg, t_Wg):
        self.C, self.cm, self.D = C, cm, D
        self.S = C.S
        self.DC = D // 128
        self.XTt, self.t_XTt, self.Wg, self.t_Wg = XTt, t_XTt, Wg, t_Wg
        self.xi = 0
        self.wi = 0
        self.pi = 0
        self.stg = [C.sb(f"pstg{i}", [128, 512], BF16) for i in range(2)]
        self.t_stg = [Tok(), Tok()]
        self.t1 = [C.sb(f"pt1_{i}", [128, 512], F32) for i in range(2)]
        self.t_t1 = [Tok(), Tok()]
        self.si = 0

    def load_w(self, w_d, c0, ncols):
        S = self.S
        i = self.wi % 2
        self.wi += 1
        wv = w_d.rearrange("(c p) n -> p c n", p=128)
        npad = (ncols + 127) // 128 * 128
        Wv = self.Wg[i][:, 0:self.DC * npad].rearrange("p (c n) -> p c n", n=npad)[:, :, 0:ncols]
        DMA(S, "pool", Wv, wv[:, :, c0:c0 + ncols], f"wg{i}", writes=[self.t_Wg[i]])
        return Wv, self.t_Wg[i]

    def load_x(self, xT_d, t_xT_d, r, T):
        S = self.S
        i = self.xi % 2
        self.xi += 1
        DMA(S, "sp", self.XTt[i][:, :, 0:T], xT_d[:, :, r:r + T], f"xtl{i}", reads=[t_xT_d], writes=[self.t_XTt[i]])
        return self.XTt[i], self.t_XTt[i]

    def fm(self, Wv, tW, c0, X, tX, T, bank):
        S = self.S
        ps, t_ps = self.cm["ps"], self.cm["t_ps"]
        for dc in range(self.DC):
            mm(S, ps[bank][:, 0:T], Wv[:, dc, c0:c0 + 128], X[:, dc, 0:T], dc == 0, dc == self.DC - 1, reads=[tW, tX], writes=[t_ps[bank]], inc=(dc == self.DC - 1))
        return ps[bank][:, 0:T], t_ps[bank]

    def tm(self, Wv, tW, c0, ncols, X, tX, ts, bank):
        S = self.S
        ps, t_ps = self.cm["ps"], self.cm["t_ps"]
        for dc in range(self.DC):
            mm(S, ps[bank][:, 0:ncols], X[:, dc, ts * 128:(ts + 1) * 128], Wv[:, dc, c0:c0 + ncols], dc == 0, dc == self.DC - 1, reads=[tW, tX], writes=[t_ps[bank]], inc=(dc == self.DC - 1))
        return ps[bank][:, 0:ncols], t_ps[bank]

    def rope_pair(self, Wv, tW, c0, X, tX, T, rt, dst_ap, t_dst):
        S = self.S
        b = (self.pi % 2) * 2
        self.pi += 1
        A, tA = self.fm(Wv, tW, c0, X, tX, T, b)
        B, tB = self.fm(Wv, tW, c0 + 128, X, tX, T, b + 1)
        k = self.si % 2
        self.si += 1
        t1, tt1 = self.t1[k][:, 0:T], self.t_t1[k]
        sg, tsg = self.stg[k][:, 0:T], self.t_stg[k]
        V(S, "dve", "tensor_tensor", t1, A, rt["Ct"][:, 0:T], ALU.mult, reads=[tA, rt["t_tab"]], writes=[tt1])
        V(S, "dve", "tensor_tensor", sg, B, rt["St"][:, 0:T], ALU.mult, reads=[tB, rt["t_tab"]], writes=[tsg])
        V(S, "pool", "tensor_tensor", sg, sg, t1, ALU.add, reads=[tt1, tsg], writes=[tsg])
        DMA(S, "sp", dst_ap, sg if dst_ap.ndim == 2 else sg.rearrange("p (a b) -> p a b", b=128), f"pstg{k}", reads=[tsg], pwrites=[t_dst])

    def fm_plain(self, Wv, tW, c0, X, tX, T, dst_ap, t_dst):
        S = self.S
        b = (self.pi % 2) * 2
        self.pi += 1
        A, tA = self.fm(Wv, tW, c0, X, tX, T, b)
        k = self.si % 2
        self.si += 1
        sg, tsg = self.stg[k][:, 0:T], self.t_stg[k]
        V(S, "act", "copy", sg, A, reads=[tA], writes=[tsg])
        DMA(S, "sp", dst_ap, sg, f"pstg{k}", reads=[tsg], pwrites=[t_dst])


def tiles_of(ntok, first):
    out = []
    r = 0
    if first:
        out.append((0, first))
        r = first
    while r < ntok:
        out.append((r, min(512, ntok - r)))
        r += 512
    return out


class Attn:
    def __init__(self, C, cm):
        self.C, self.cm, self.S = C, cm, C.S
        self.Pt = [C.sb(f"Pt{i}", [128, 384], BF16) for i in range(3)]
        self.t_Pt = [Tok() for _ in range(3)]
        self.pi = 0
        self.si = 0

    def step(self, Klhs, tK, Qt, tQ, extra, mulmask, tmask, pv, first):
        S, cm = self.S, self.cm
        ps, t_ps = cm["ps"], cm["t_ps"]
        b = self.si % 2
        self.si += 1
        mm(S, ps[b][:, 0:384], Klhs, Qt, True, extra is None, reads=[tK, tQ], writes=[t_ps[b]], inc=(extra is None))
        if extra is not None:
            mm(S, ps[b][:, 0:384], extra[0], extra[1], False, True, reads=[extra[2]], writes=[t_ps[b]], inc=True)
        i = self.pi % 3
        self.pi += 1
        P, tP = self.Pt[i], self.t_Pt[i]
        V(S, "act", "activation", P[:, :], ps[b][:, 0:384], AF.Exp, scale=SCALE, reads=[t_ps[b]], writes=[tP])
        if mulmask is not None:
            V(S, "pool", "tensor_tensor", P[:, :], P[:, :], mulmask, ALU.mult, reads=[tP, tmask], writes=[tP])
        for (bank, width, rhs, t_rhs) in pv:
            for g in range(3):
                st = first and g == 0
                mm(S, ps[bank][:, g * width:(g + 1) * width], P[:, g * 128:(g + 1) * 128], rhs, st, False, reads=[tP, t_rhs], writes=[t_ps[bank]], inc=(g == 2),
                   skip_group_check=True)


def nsa_proj(C, cm, xT_d, t_xT_d, pos_d, w_d, gateb_d, sc, gates, t_gates, D, NTOK):
    S = C.S
    with ExitStack() as es:
        C2 = Ctx.__new__(Ctx)
        C2.nc, C2.es, C2.S, C2.n = C.nc, es, C.S, C.n + 1000
        DC = D // 128
        XTt = [C2.sb(f"XTt{i}", [128, DC, 512], BF16) for i in range(2)]
        Wg = [C2.sb(f"Wg{i}", [128, DC * 512], BF16) for i in range(2)]
        P = Proj(C2, cm, D, XTt, [Tok(), Tok()], Wg, [Tok(), Tok()])
        rt = alloc_rope(C2)
        gb = C2.sb("gateb", [128, 9], F32)
        t_gb = Tok()
        DMA(S, "sp", gb[:, :], gateb_d[0:9].partition_broadcast(128), "gateb", writes=[t_gb])
        vst_ = [C2.sb(f"vst{i}", [128, 2, 132], BF16) for i in range(2)]
        vst = [v_[:, :, 0:129] for v_ in vst_]
        t_vst = [Tok(), Tok()]
        for i in range(2):
            V(S, "pool", "memset", vst[i][:, :, 128:129], 1.0, writes=[t_vst[i]])
        gtmp = C2.sb("gtmp", [128, 9], F32)
        t_gtmp = Tok()
        tiles = tiles_of(NTOK, 0)
        t_sc = sc["t"]
        groups = [
            (0, 512, [("q", 0, 0), ("q", 256, 1)]),
            (512, 512, [("q", 0, 2), ("k", 256, "kcT")]),
            (1024, 512, [("k", 0, "kslT"), ("k", 256, "kwT")]),
        ]
        for (c0, ncols, units) in groups:
            Wv, tW = P.load_w(w_d, c0, ncols)
            for (r, T) in tiles:
                X, tX = P.load_x(xT_d, t_xT_d, r, T)
                rope_tables(C2, cm, rt, pos_d, r, T)
                for (kind, off, dst) in units:
                    if kind == "q":
                        dap = sc["QT"][:, r // 128:(r + T) // 128, dst, :]
                    else:
                        dap = sc[dst][:, r:r + T]
                    P.rope_pair(Wv, tW, off, X, tX, T, rt, dap, t_sc)
        Wv, tW = P.load_w(w_d, 1536, 400)
        vi = 0
        for (r, T) in (tiles if DBG["level"] >= -1 else ()):
            X, tX = P.load_x(xT_d, t_xT_d, r, T)
            P.fm_plain(Wv, tW, 0, X, tX, T, sc["vcT"][:, r:r + T], t_sc)
            for ts in range(T // 128):
                bank = 4 + (vi % 2)
                k = vi % 2
                vi += 1
                A, tA = P.tm(Wv, tW, 128, 272, X, tX, ts, bank)
                qi = r // 128 + ts
                V(S, "act", "copy", vst[k][:, :, 0:128], A[:, 0:256].rearrange("p (a b) -> p a b", b=128), reads=[tA], pwrites=[t_vst[k]])
                DMA(S, "sp", sc["vsl"][:, qi, :], vst[k][:, 0, :], f"vst{k}", reads=[t_vst[k]], pwrites=[t_sc])
                DMA(S, "sp", sc["vw"][:, qi, :], vst[k][:, 1, :], f"vst{k}", reads=[t_vst[k]], pwrites=[t_sc])
                V(S, "dve", "tensor_tensor", gtmp[:, :], A[:, 256:265], gb[:, :], ALU.add, reads=[tA, t_gb, t_vst[k]], writes=[t_gtmp])
                V(S, "act", "activation", gates[:, qi, :], gtmp[:, :], AF.Sigmoid, reads=[t_gtmp], pwrites=[t_gates])
        C.n = C2.n
    barrier(S)


def load_cast(S, dst, src, chan, t):
    DMA(S, "pool", dst, src, chan, pwrites=[t])


def nsa_attn(C, cm, sc, gates, t_gates, cst, cmp_pos_d, cmp_w1_d, cmp_w2_d, out_d, t_out, NTOK):
    S = C.S
    NQ = NTOK // 128
    ps, t_ps = cm["ps"], cm["t_ps"]
    if DBG["level"] < 0:
        return
    with ExitStack() as es:
        C2 = Ctx.__new__(Ctx)
        C2.nc, C2.es, C2.S, C2.n = C.nc, es, C.S, C.n + 1000
        t_sc = sc["t"]
        tK = Tok()
        kslT = C2.sb("kslT", [128, NTOK], BF16)
        kwT = C2.sb("kwT", [128, NTOK], BF16)
        kcT = C2.sb("kcT", [128, NTOK], BF16)
        vcT = C2.sb("vcT", [128, NTOK], BF16)
        vsl = C2.sb("vsl", [128, NQ, 129], BF16)
        vw = C2.sb("vw", [128, NQ, 129], BF16)
        for (dst, src, ch) in ((kslT, sc["kslT"], "l0"), (kwT, sc["kwT"], "l1"), (kcT, sc["kcT"], "l2"), (vcT, sc["vcT"], "l3")):
            DMA(S, "sp", dst[:, :], src[:, :], ch, reads=[t_sc], pwrites=[tK])
        DMA(S, "sp", vsl[:, :, :], sc["vsl"][:, :, :], "l4", reads=[t_sc], pwrites=[tK])
        DMA(S, "sp", vw[:, :, :], sc["vw"][:, :, :], "l5", reads=[t_sc], pwrites=[tK])
        tc_ = Tok()
        esel = C2.sb("esel", [128, NTOK], BF16)
        cmask = C2.sb("cmask", [128, 17, 384], BF16)
        caus = C2.sb("caus", [128, 384], BF16)
        upper = C2.sb("upper", [128, 384], BF16)
        ovl = C2.sb("ovl", [128, 4, 128], BF16)
        fpat = C2.sb("fpat", [128, 3], F32)
        load_cast(S, esel[:, :], cst["esel"][:, :], "c0", tc_)
        load_cast(S, cmask[:, :, :], cst["cmask"][:, :, :], "c1", tc_)
        load_cast(S, caus[:, :], cst["caus"][:, :], "c2", tc_)
        load_cast(S, upper[:, :], cst["upper"][:, :], "c3", tc_)
        load_cast(S, ovl[:, :, :], cst["ovl"][:, :, :], "c4", tc_)
        DMA(S, "sp", fpat[:, :], cst["fpat"][:, :], "c5", pwrites=[tc_])
        kcmpT = C2.sb("kcmpT", [128, 512], BF16)
        vcaug = C2.sb("vcaug", [128, 4, 129], BF16)
        t_cmp = Tok()
        W1 = C2.sb("W1", [128, 32, 128], BF16)
        W2 = C2.sb("W2", [128, 128], BF16)
        posT = C2.sb("posT", [128, 32], BF16)
        posN = C2.sb("posN", [32, 128], F32)
        t_pn = Tok()
        biasc = C2.sb("biasc", [128, 1], F32)
        Acm = C2.sb("Acm", [128, 512], BF16)
        t_w = Tok()
        t_A = Tok()
        V(S, "pool", "memset", vcaug[:, :, 128:129], 1.0, pwrites=[t_cmp])
        for which, src in (((0, kcT), (1, vcT)) if DBG["level"] >= 1 else ()):
            S.op("pool", (lambda which=which: lambda e: e.dma_start(out=W1[:, :, :], in_=cmp_w1_d[which].rearrange("(l d) h -> d l h", d=128)))(), writes=[t_w], dma="cw1")
            S.op("pool", (lambda which=which: lambda e: e.dma_start(out=W2[:, :], in_=cmp_w2_d[which]))(), pwrites=[t_w], dma="cw2")
            S.op("sp", (lambda which=which: lambda e: e.dma_start(out=posN[0:32, :], in_=cmp_pos_d[which]))(), writes=[t_pn], dma="cw3")
            S.op("pe", lambda e: e.transpose(ps[6][:, 0:32], posN[0:32, :], cm["ident"][0:32, 0:32]), reads=[t_pn, cm["t_const"]], writes=[t_ps[6]])
            V(S, "dve", "tensor_copy", posT[:, :], ps[6][:, 0:32], reads=[t_ps[6]], pwrites=[t_w])
            for l in range(32):
                mm(S, ps[6][:, 0:1], W1[:, l, :], posT[:, l:l + 1], l == 0, l == 31, reads=[t_w], writes=[t_ps[6]], inc=(l == 31))
            V(S, "dve", "tensor_copy", biasc[:, :], ps[6][:, 0:1], reads=[t_ps[6]], writes=[t_A])
            for l in range(32):
                mm(S, ps[7][:, 0:511], W1[:, l, :], src[:, l:l + 16 * 510 + 1:16], l == 0, l == 31, reads=[t_w, tK], writes=[t_ps[7]], inc=(l == 31))
            V(S, "pool", "memset", Acm[:, 511:512], 0.0, writes=[t_A])
            V(S, "act", "activation", Acm[:, 0:511], ps[7][:, 0:511], AF.Silu, bias=biasc[:, 0:1], reads=[t_ps[7], t_A], writes=[t_A])
            if which == 0:
                mm(S, ps[6][:, 0:512], W2[:, :], Acm[:, :], True, True, reads=[t_w, t_A], writes=[t_ps[6]], inc=True)
                V(S, "dve", "tensor_copy", kcmpT[:, :], ps[6][:, 0:512], reads=[t_ps[6]], pwrites=[t_cmp])
            else:
                for m in range(4):
                    mm(S, ps[6][:, m * 128:(m + 1) * 128], Acm[:, m * 128:(m + 1) * 128], W2[:, :], True, True, reads=[t_w, t_A], writes=[t_ps[6]], inc=(m == 3))
                V(S, "dve", "tensor_copy", vcaug[:, :, 0:128], ps[6][:, 0:512].rearrange("p (a b) -> p a b", b=128), reads=[t_ps[6]], pwrites=[t_cmp])
        AT = Attn(C2, cm)
        Qt = [C2.sb(f"Qt{i}", [128, 384], BF16) for i in range(2)]
        t_Qt = [Tok(), Tok()]
        imp = C2.sb("imp", [128, 128], F32)
        imp2 = C2.sb("imp2", [128, 128], F32)
        mx = C2.sb("mx", [128, 8], F32)
        mx2 = C2.sb("mx2", [128, 8], F32)
        mneg = C2.sb("mneg", [128, 128], F32)
        mnegT = C2.sb("mnegT", [128, 384], BF16)
        den = C2.sb("den", [128, 9], F32)
        coef = C2.sb("coef", [128, 9], F32)
        rdc = C2.sb("rdc", [128, 3], F32)
        ot = [C2.sb(f"ot{i}", [128, 384], F32) for i in range(2)]
        ob = [C2.sb(f"ob{i}", [128, 384], BF16) for i in range(2)]
        t_ot = [Tok(), Tok()]
        t_i = Tok()
        t_mT = Tok()
        t_d = Tok()
        for i in range(NQ if DBG["nq"] is None else DBG["nq"]):
            if DBG["level"] < 2:
                break
            q = i % 2
            DMA(S, "sp", Qt[q][:, :], sc["QT"][:, i, :, :].rearrange("p a b -> p (a b)"), f"qt{q}", reads=[t_sc], writes=[t_Qt[q]])
            Q, tQ = Qt[q][:, :], t_Qt[q]
            mlist = [m for m in range(4) if 128 * m <= 8 * i + 6]
            for n_, m in enumerate(mlist):
                dlt = i - 16 * m
                mask = cmask[:, dlt, :] if dlt <= 16 else None
                AT.step(kcmpT[:, m * 128:(m + 1) * 128], t_cmp, Q, tQ, None, mask, tc_,
                        [(2, 129, vcaug[:, m, :], t_cmp), (3, 128, ovl[:, m, :], tc_)], n_ == 0)
            if DBG["level"] < 3:
                continue
            V(S, "dve", "tensor_scalar", rdc[:, :], ps[2][:, 0:387].rearrange("p (g c) -> p g c", c=129)[:, :, 128], 1e-30, None, ALU.max, reads=[t_ps[2]], writes=[t_d])
            V(S, "dve", "reciprocal", rdc[:, :], rdc[:, :], reads=[t_d], writes=[t_d])
            V(S, "dve", "tensor_scalar", imp[:, :], ps[3][:, 0:128], rdc[:, 0:1], None, ALU.mult, reads=[t_ps[3], t_d], writes=[t_i])
            for g in (1, 2):
                V(S, "dve", "scalar_tensor_tensor", imp[:, :], ps[3][:, g * 128:(g + 1) * 128], rdc[:, g:g + 1], imp[:, :], ALU.mult, ALU.add, reads=[t_ps[3], t_d, t_i], writes=[t_i])
            lo = max(0, 2 * i - 1)
            hi = min(128, 2 * i + 2)
            V(S, "dve", "tensor_tensor", imp[:, lo:hi], imp[:, lo:hi], fpat[:, lo - (2 * i - 1):hi - (2 * i - 1)], ALU.add, reads=[t_i, tc_], writes=[t_i])
            V(S, "dve", "memset", imp[:, 0:1], 3e4, writes=[t_i])
            V(S, "dve", "max", mx[:, :], imp[:, :], reads=[t_i], writes=[t_d])
            V(S, "dve", "match_replace", imp2[:, :], mx[:, :], imp[:, :], -1e9, reads=[t_i, t_d], writes=[t_d])
            V(S, "dve", "max", mx2[:, :], imp2[:, :], reads=[t_d], writes=[t_d])
            V(S, "dve", "tensor_scalar", mneg[:, :], imp[:, :], mx2[:, 7:8], NEGM, ALU.is_lt, ALU.mult, reads=[t_i, t_d], writes=[t_i])
            S.op("pe", lambda e: e.transpose(ps[6][:, 0:128], mneg[:, :], cm["ident"][:]), reads=[t_i, cm["t_const"]], writes=[t_ps[6]])
            V(S, "act", "copy", mnegT[:, :].rearrange("p (g c) -> p g c", c=128), ps[6][:, 0:128].unsqueeze(1).to_broadcast([128, 3, 128]), reads=[t_ps[6]], writes=[t_mT])
            if DBG["level"] < 4:
                continue
            for kt in range(i + 1):
                AT.step(kslT[:, kt * 128:(kt + 1) * 128], tK, Q, tQ, (esel[:, kt * 128:(kt + 1) * 128], mnegT[:, :], t_mT),
                        caus[:, :] if kt == i else None, tc_, [(4, 129, vsl[:, kt, :], tK)], kt == 0)
            if DBG["level"] < 5:
                continue
            k0 = max(0, i - 4)
            for kt in range(k0, i + 1):
                mask = caus[:, :] if kt == i else (upper[:, :] if kt == i - 4 else None)
                AT.step(kwT[:, kt * 128:(kt + 1) * 128], tK, Q, tQ, None, mask, tc_, [(5, 129, vw[:, kt, :], tK)], kt == k0)
            if DBG["level"] < 6:
                continue
            for br, bank in enumerate((2, 4, 5)):
                V(S, "dve", "tensor_scalar", den[:, :].rearrange("p (g b) -> p g b", b=3)[:, :, br], ps[bank][:, 0:387].rearrange("p (g c) -> p g c", c=129)[:, :, 128], 1e-30, None, ALU.max,
                  reads=[t_ps[bank]], writes=[t_d])
            V(S, "dve", "reciprocal", den[:, :], den[:, :], reads=[t_d], writes=[t_d])
            V(S, "dve", "tensor_tensor", coef[:, :], den[:, :], gates[:, i, :], ALU.mult, reads=[t_d, t_gates], writes=[t_d])
            o, tO = ot[q], t_ot[q]
            for g in range(3):
                og = o[:, g * 128:(g + 1) * 128]
                V(S, "dve", "tensor_scalar", og, ps[2][:, g * 129:g * 129 + 128], coef[:, g * 3:g * 3 + 1], None, ALU.mult, reads=[t_ps[2], t_d], writes=[tO])
                V(S, "dve", "scalar_tensor_tensor", og, ps[4][:, g * 129:g * 129 + 128], coef[:, g * 3 + 1:g * 3 + 2], og, ALU.mult, ALU.add, reads=[t_ps[4], t_d, tO], writes=[tO])
                V(S, "dve", "scalar_tensor_tensor", ob[q][:, g * 128:(g + 1) * 128], ps[5][:, g * 129:g * 129 + 128], coef[:, g * 3 + 2:g * 3 + 3], og, ALU.mult, ALU.add, reads=[t_ps[5], t_d, tO], writes=[tO])
            DMA(S, "sp", out_d[i * 128:(i + 1) * 128, :], ob[q][:, :], f"ob{q}", reads=[tO], pwrites=[t_out])
        C.n = C2.n
    barrier(S)


def sub_ctx_unused(C, es):
    C2 = Ctx.__new__(Ctx)
    C2.nc, C2.es, C2.S, C2.n = C.nc, es, C.S, C.n + 1000
    return C2


def swa_proj(C, cm, xT_d, t_xT_d, pos_d, w_d, sc, D, NTOK):
    S = C.S
    with ExitStack() as es:
        C2 = sub_ctx(C, es)
        DC = D // 128
        XTt = [C2.sb(f"XTt{i}", [128, DC, 512], BF16) for i in range(2)]
        Wg = [C2.sb(f"Wg{i}", [128, DC * 512], BF16) for i in range(2)]
        P = Proj(C2, cm, D, XTt, [Tok(), Tok()], Wg, [Tok(), Tok()])
        rt = alloc_rope(C2)
        vst = [C2.sb(f"vst{i}", [128, 129], BF16) for i in range(2)]
        t_vst = [Tok(), Tok()]
        for i in range(2):
            V(S, "pool", "memset", vst[i][:, 128:129], 1.0, writes=[t_vst[i]])
        tiles = tiles_of(NTOK, 128)
        t_sc = sc["t"]
        for gi in range(3):
            Wv, tW = P.load_w(w_d, gi * 512, 512)
            for (r, T) in tiles[1:]:
                X, tX = P.load_x(xT_d, t_xT_d, r, T)
                rope_tables(C2, cm, rt, pos_d, r, T)
                for u in range(2):
                    qi0 = (r - 128) // 128
                    P.rope_pair(Wv, tW, u * 256, X, tX, T, rt, sc["QT"][:, qi0:qi0 + T // 128, gi * 2 + u, :], t_sc)
        Wv, tW = P.load_w(w_d, 1536, 384)
        vi = 0
        for (r, T) in tiles:
            X, tX = P.load_x(xT_d, t_xT_d, r, T)
            rope_tables(C2, cm, rt, pos_d, r, T)
            P.rope_pair(Wv, tW, 0, X, tX, T, rt, sc["kT"][:, r:r + T], t_sc)
            for ts in range(T // 128):
                bank = 4 + (vi % 2)
                k = vi % 2
                vi += 1
                A, tA = P.tm(Wv, tW, 256, 128, X, tX, ts, bank)
                V(S, "act", "copy", vst[k][:, 0:128], A, reads=[tA], pwrites=[t_vst[k]])
                DMA(S, "sp", sc["v"][:, r // 128 + ts, :], vst[k][:, :], f"vst{k}", reads=[t_vst[k]], pwrites=[t_sc])
        C.n = C2.n
    barrier(S)


def swa_attn(C, cm, sc, cst, sinks_d, mask0_d, out_d, t_out, NTOK):
    S = C.S
    NQ = (NTOK - 128) // 128
    ps, t_ps = cm["ps"], cm["t_ps"]
    with ExitStack() as es:
        C2 = sub_ctx(C, es)
        t_sc = sc["t"]
        tK = Tok()
        kT = C2.sb("kT", [128, NTOK], BF16)
        v = C2.sb("v", [128, NQ + 1, 129], BF16)
        DMA(S, "sp", kT[:, :], sc["kT"][:, :], "l0", reads=[t_sc], pwrites=[tK])
        DMA(S, "sp", v[:, :, :], sc["v"][:, :, :], "l1", reads=[t_sc], pwrites=[tK])
        tc_ = Tok()
        caus = C2.sb("caus", [128, 384], BF16)
        upper = C2.sb("upper", [128, 384], BF16)
        mask0 = C2.sb("mask0", [128, 384], BF16)
        load_cast(S, caus[:, :], cst["caus"][:, :], "c2", tc_)
        load_cast(S, upper[:, :], cst["upper"][:, :], "c3", tc_)
        load_cast(S, mask0[:, :], mask0_d[:, :], "c4", tc_)
        esink = C2.sb("esink", [128, 6], F32)
        DMA(S, "sp", esink[:, :], sinks_d[0:6].partition_broadcast(128), "c5", writes=[tc_])
        V(S, "act", "activation", esink[:, :], esink[:, :], AF.Exp, reads=[tc_], writes=[tc_])
        AT = Attn(C2, cm)
        Qt = [C2.sb(f"Qt{i}", [128, 384], BF16) for i in range(2)]
        t_Qt = [Tok(), Tok()]
        den = C2.sb("den", [128, 3], F32)
        t_d = Tok()
        ob = [C2.sb(f"ob{i}", [128, 768], BF16) for i in range(2)]
        t_ob = [Tok(), Tok()]
        qq = 0
        for i in range(NQ):
            oq = i % 2
            for hg in range(2):
                q = qq % 2
                qq += 1
                DMA(S, "sp", Qt[q][:, :], sc["QT"][:, i, 3 * hg:3 * hg + 3, :].rearrange("p a b -> p (a b)"), f"qt{q}", reads=[t_sc], writes=[t_Qt[q]])
                Q, tQ = Qt[q][:, :], t_Qt[q]
                AT.step(kT[:, i * 128:(i + 1) * 128], tK, Q, tQ, None, mask0[:, :] if i == 0 else upper[:, :], tc_, [(4, 129, v[:, i, :], tK)], True)
                AT.step(kT[:, (i + 1) * 128:(i + 2) * 128], tK, Q, tQ, None, caus[:, :], tc_, [(4, 129, v[:, i + 1, :], tK)], False)
                V(S, "dve", "tensor_tensor", den[:, :], ps[4][:, 0:387].rearrange("p (g c) -> p g c", c=129)[:, :, 128], esink[:, 3 * hg:3 * hg + 3], ALU.add, reads=[t_ps[4], tc_], writes=[t_d])
                V(S, "dve", "reciprocal", den[:, :], den[:, :], reads=[t_d], writes=[t_d])
                for g in range(3):
                    c0 = (3 * hg + g) * 128
                    V(S, "dve", "tensor_scalar", ob[oq][:, c0:c0 + 128], ps[4][:, g * 129:g * 129 + 128], den[:, g:g + 1], None, ALU.mult, reads=[t_ps[4], t_d], pwrites=[t_ob[oq]] if (hg, g) != (0, 0) else (),
                      writes=[t_ob[oq]] if (hg, g) == (0, 0) else ())
            DMA(S, "sp", out_d[i * 128:(i + 1) * 128, :], ob[oq][:, :], f"ob{oq}", reads=[t_ob[oq]], pwrites=[t_out])
        C.n = C2.n
    barrier(S)


def conv_phase(C, cm, xT_d, t_xT_d, w_d, dwT_d, dwb_d, lng_d, lnb_d, yc_d, out_d, t_out, D, NTOK):
    S = C.S
    NOWN = NTOK - 128
    ps, t_ps = cm["ps"], cm["t_ps"]
    with ExitStack() as es:
        C2 = sub_ctx(C, es)
        DC = D // 128
        XTt = [C2.sb(f"XTt{i}", [128, DC, 512], BF16) for i in range(2)]
        Wg = [C2.sb(f"Wg{i}", [128, DC * 256], BF16) for i in range(2)]
        P = Proj(C2, cm, D, XTt, [Tok(), Tok()], Wg, [Tok(), Tok()])
        u = [C2.sb(f"u{i}", [128, 30 + NOWN], F32) for i in range(2)]
        t_u = [Tok(), Tok()]
        y = [C2.sb(f"y{i}", [128, NOWN], F32) for i in range(2)]
        t_y = [Tok(), Tok()]
        sig = [C2.sb(f"sig{i}", [128, 512], F32) for i in range(2)]
        t_sig = [Tok(), Tok()]
        dw = C2.sb("dw", [128, 8, 31], F32)
        dwb = C2.sb("dwb", [128, 8], F32)
        tcw = Tok()
        DMA(S, "sp", dw[:, :, :], dwT_d.rearrange("(g p) j -> p g j", p=128), "cv0", pwrites=[tcw])
        S.op("sp", lambda e: e.dma_start(out=dwb[:, :], in_=dwb_d.rearrange("(g p) -> p g", p=128), allow_slow_non_contiguous=True), pwrites=[tcw], dma="cv1")
        yst = [C2.sb(f"yst{i}", [128, 512], F32) for i in range(2)]
        t_yst = [Tok(), Tok()]
        t_yc = Tok()
        tiles = tiles_of(NTOK, 128)
        si = 0
        yi = 0
        for cg in range(8):
            k = cg % 2
            eng = "dve"
            i = P.wi % 2
            P.wi += 1
            wv = w_d.rearrange("(c p) n -> p c n", p=128)
            Wv = Wg[i][:, 0:DC * 256].rearrange("p (c n) -> p c n", n=256)
            DMA(S, "pool", Wv[:, :, 0:128], wv[:, :, cg * 128:(cg + 1) * 128], f"wg{i}", writes=[P.t_Wg[i]])
            DMA(S, "pool", Wv[:, :, 128:256], wv[:, :, 1024 + cg * 128:1024 + (cg + 1) * 128], f"wg{i}", pwrites=[P.t_Wg[i]])
            tW = P.t_Wg[i]
            for (r, T) in tiles:
                X, tX = P.load_x(xT_d, t_xT_d, r, T)
                b = (P.pi % 2) * 2
                P.pi += 1
                A, tA = P.fm(Wv, tW, 0, X, tX, T, b)
                G, tG = P.fm(Wv, tW, 128, X, tX, T, b + 1)
                s = si % 2
                si += 1
                V(S, "act", "activation", sig[s][:, 0:T], G, AF.Sigmoid, reads=[tG], writes=[t_sig[s]])
                if r == 0:
                    V(S, "dve", "tensor_tensor", u[k][:, 0:30], A[:, 98:128], sig[s][:, 98:128], ALU.mult, reads=[tA, t_sig[s]], writes=[t_u[k]])
                else:
                    V(S, "dve", "tensor_tensor", u[k][:, 30 + r - 128:30 + r - 128 + T], A, sig[s][:, 0:T], ALU.mult, reads=[tA, t_sig[s]], pwrites=[t_u[k]])
            V(S, eng, "tensor_scalar", y[k][:, :], u[k][:, 0:NOWN], dw[:, cg, 0:1], dwb[:, cg:cg + 1], ALU.mult, ALU.add, reads=[t_u[k], tcw], writes=[t_y[k]])
            for j in range(1, 31):
                V(S, eng, "scalar_tensor_tensor", y[k][:, :], u[k][:, j:j + NOWN], dw[:, cg, j:j + 1], y[k][:, :], ALU.mult, ALU.add, reads=[t_u[k], tcw, t_y[k]], writes=[t_y[k]])
            for t4 in range(NOWN // 512):
                bank = 6 + (yi % 2)
                ys = yi % 2
                yi += 1
                for j in range(4):
                    tt_ = t4 * 4 + j
                    S.op("pe", (lambda bank=bank, j=j, k=k, tt_=tt_: lambda e: e.transpose(ps[bank][:, j * 128:(j + 1) * 128], y[k][:, tt_ * 128:(tt_ + 1) * 128], cm["ident"][:]))(),
                         reads=[t_y[k], cm["t_const"]], writes=[t_ps[bank]], inc=(j == 3))
                V(S, "act", "copy", yst[ys][:, :], ps[bank][:, 0:512], reads=[t_ps[bank]], writes=[t_yst[ys]])
                DMA(S, "sp", yc_d[t4 * 512:(t4 + 1) * 512, cg * 128:(cg + 1) * 128].rearrange("(a p) c -> p a c", p=128), yst[ys][:, :].rearrange("p (a c) -> p a c", c=128),
                    f"yst{ys}", reads=[t_yst[ys]], pwrites=[t_yc])
        G_ = C2.sb("lnG", [128, 1024], F32)
        B_ = C2.sb("lnB", [128, 1024], F32)
        DMA(S, "sp", G_[:, :], lng_d[0:1024].partition_broadcast(128), "cv2", pwrites=[tcw])
        DMA(S, "sp", B_[:, :], lnb_d[0:1024].partition_broadcast(128), "cv3", pwrites=[tcw])
        yr = [C2.sb(f"yr{i}", [128, 1024], F32) for i in range(2)]
        t_yr = [Tok(), Tok()]
        yo = [C2.sb(f"yo{i}", [128, 1024], BF16) for i in range(2)]
        t_yo = [Tok(), Tok()]
        tmp = cm["tmp"]
        ts_ = tmp["t_small"]
        for tt_ in range(NOWN // 128):
            k = tt_ % 2
            DMA(S, "sp", yr[k][:, :], yc_d[tt_ * 128:(tt_ + 1) * 128, :], f"yr{k}", reads=[t_yc], writes=[t_yr[k]])
            for h in range(2):
                V(S, "dve", "bn_stats", tmp["stats"][:, h, :], yr[k][:, h * 512:(h + 1) * 512], reads=[t_yr[k]], writes=[ts_])
            V(S, "dve", "bn_aggr", tmp["mv"][:, :], tmp["stats"][:, 0:2, :], reads=[ts_], writes=[ts_])
            V(S, "act", "activation", tmp["sd"][:, :], tmp["mv"][:, 1:2], AF.Sqrt, bias=tmp["eps"][:, 0:1], scale=1.0, reads=[ts_], writes=[ts_])
            V(S, "dve", "reciprocal", tmp["rstd"][:, :], tmp["sd"][:, :], reads=[ts_], writes=[ts_])
            V(S, "dve", "tensor_scalar", yr[k][:, :], yr[k][:, :], tmp["mv"][:, 0:1], tmp["rstd"][:, 0:1], ALU.subtract, ALU.mult, reads=[ts_, t_yr[k]], writes=[t_yr[k]])
            V(S, "pool", "tensor_tensor", yr[k][:, :], yr[k][:, :], G_[:, :], ALU.mult, reads=[t_yr[k], tcw], writes=[t_yr[k]])
            V(S, "pool", "tensor_tensor", yr[k][:, :], yr[k][:, :], B_[:, :], ALU.add, reads=[t_yr[k], tcw], writes=[t_yr[k]])
            V(S, "act", "activation", yo[k][:, :], yr[k][:, :], AF.Silu, reads=[t_yr[k]], writes=[t_yo[k]])
            DMA(S, "sp", out_d[tt_ * 128:(tt_ + 1) * 128, :], yo[k][:, :], f"yo{k}", reads=[t_yo[k]], pwrites=[t_out])
        C.n = C2.n
    barrier(S)

from concourse.bass_utils import run_bass_kernel_spmd

D_MODEL = 4096
D_FF = 11008
SEQ = 8192
NCORES = 8
NTC = 2048
ALPHA = 4 ** 0.25
EPS = 1e-5
_PROGS = {}


def _din(nc, name, shape, dt=F32):
    return nc.dram_tensor(name, list(shape), dt, kind="ExternalInput").ap()


def _dout(nc, name, shape, dt=F32):
    return nc.dram_tensor(name, list(shape), dt, kind="ExternalOutput").ap()


def _dint(nc, name, shape, dt=F32):
    return nc.dram_tensor(name, list(shape), dt, kind="Internal").ap()


def build_ffn_prog():
    nc = bass.Bass("TRN2", target_bir_lowering=False)
    x = _din(nc, "x", [NTC, D_MODEL])
    wgu = _din(nc, "wgu", [D_MODEL, 2 * D_FF])
    wd = _din(nc, "wd", [D_FF, D_MODEL])
    g = _din(nc, "g", [D_MODEL])
    b = _din(nc, "b", [D_MODEL])
    ident = _din(nc, "ident", [128, 128])
    z = _dint(nc, "z", [NTC, D_MODEL])
    y = _dout(nc, "y", [NTC, D_MODEL])
    with ExitStack() as es:
        C = Ctx(nc, es)
        cm = alloc_common(C, D_MODEL)
        init_common(C, cm, ident)
        ffn_stage(C, cm, x, wgu, wd, g, b, z, y, NTC, D_MODEL, D_FF, ALPHA, EPS)
        C.S.emit()
    return nc


def build_c_prog():
    nc = bass.Bass("TRN2", target_bir_lowering=False)
    mixT = _din(nc, "mixT", [D_MODEL, NTC], BF16)
    x = _din(nc, "x", [NTC, D_MODEL])
    wo = _din(nc, "wo", [D_MODEL, D_MODEL])
    g2 = _din(nc, "g2", [D_MODEL])
    b2 = _din(nc, "b2", [D_MODEL])
    wgu = _din(nc, "wgu", [D_MODEL, 2 * D_FF])
    wd = _din(nc, "wd", [D_FF, D_MODEL])
    g = _din(nc, "g", [D_MODEL])
    b = _din(nc, "b", [D_MODEL])
    ident = _din(nc, "ident", [128, 128])
    z = _dint(nc, "z", [NTC, D_MODEL])
    x2 = _dint(nc, "x2", [NTC, D_MODEL])
    y = _dout(nc, "y", [NTC, D_MODEL])
    with ExitStack() as es:
        C = Ctx(nc, es)
        cm = alloc_common(C, D_MODEL)
        init_common(C, cm, ident)
        with ExitStack() as e1:
            C1 = sub_ctx(C, e1)
            outproj_stage(C1, cm, mixT, x, wo, g2, b2, z, x2, NTC, D_MODEL, ALPHA, EPS)
            C.n = C1.n
        barrier(C.S)
        with ExitStack() as e2:
            C2 = sub_ctx(C, e2)
            ffn_stage(C2, cm, x2, wgu, wd, g, b, z, y, NTC, D_MODEL, D_FF, ALPHA, EPS)
            C.n = C2.n
        C.S.emit()
    return nc


N_SWA = 4096 + 128
N_CONV = 2048 + 128


def build_mixer_prog(parts=("nsa", "swa", "conv")):
    nc = bass.Bass("TRN2", target_bir_lowering=False)
    D = D_MODEL
    x_nsa = _din(nc, "x_nsa", [SEQ, D])
    pos_nsa = _din(nc, "pos_nsa", [SEQ], I32)
    w_nsa = _din(nc, "w_nsa", [D, 1936])
    gateb = _din(nc, "gateb", [9])
    cmp_pos = _din(nc, "cmp_pos", [2, 32, 128])
    cmp_w1 = _din(nc, "cmp_w1", [2, 4096, 128])
    cmp_w2 = _din(nc, "cmp_w2", [2, 128, 128])
    x_swa = _din(nc, "x_swa", [N_SWA, D])
    pos_swa = _din(nc, "pos_swa", [N_SWA], I32)
    w_swa = _din(nc, "w_swa", [D, 1920])
    sinks = _din(nc, "sinks", [6])
    mask0 = _din(nc, "mask0", [128, 384])
    x_conv = _din(nc, "x_conv", [N_CONV, D])
    w_conv = _din(nc, "w_conv", [D, 2048])
    dwT = _din(nc, "dwT", [1024, 31])
    dwb = _din(nc, "dwb", [1024])
    clng = _din(nc, "clng", [1024])
    clnb = _din(nc, "clnb", [1024])
    ident = _din(nc, "ident", [128, 128])
    inv_d = _din(nc, "inv", [128, 1])
    sgn_d = _din(nc, "sgn", [128, 1])
    cst = {
        "esel": _din(nc, "esel", [128, SEQ]), "cmask": _din(nc, "cmask", [128, 17, 384]), "caus": _din(nc, "caus", [128, 384]),
        "upper": _din(nc, "upper", [128, 384]), "ovl": _din(nc, "ovl", [128, 4, 128]), "fpat": _din(nc, "fpat", [128, 3]),
    }
    nsa_out = _dout(nc, "nsa_out", [SEQ, 384], BF16)
    swa_out = _dout(nc, "swa_out", [4096, 768], BF16)
    conv_out = _dout(nc, "conv_out", [2048, 1024], BF16)
    xT_d = _dint(nc, "xT_d", [128, 32, SEQ], BF16)
    NQ = SEQ // 128
    sc = {"QT": _dint(nc, "s_QT", [128, NQ, 3, 128], BF16), "t": Tok()}
    for n in ("kcT", "kslT", "kwT", "vcT"):
        sc[n] = _dint(nc, "s_" + n, [128, SEQ], BF16)
    sc["vsl"] = _dint(nc, "s_vsl", [128, NQ, 129], BF16)
    sc["vw"] = _dint(nc, "s_vw", [128, NQ, 129], BF16)
    sc2 = {"QT": _dint(nc, "s2_QT", [128, 32, 6, 128], BF16), "kT": _dint(nc, "s2_kT", [128, N_SWA], BF16), "v": _dint(nc, "s2_v", [128, 33, 129], BF16), "t": Tok()}
    yc_d = _dint(nc, "yc_d", [2048, 1024])
    with ExitStack() as es:
        C = Ctx(nc, es)
        S = C.S
        cm = alloc_common(C, D)
        init_common(C, cm, ident)
        cm["inv"] = C.sb("inv", [128, 1], F32)
        cm["sgn"] = C.sb("sgn", [128, 1], F32)
        DMA(S, "sp", cm["inv"][:, :], inv_d[:, :], "const", pwrites=[cm["t_const"]])
        DMA(S, "sp", cm["sgn"][:, :], sgn_d[:, :], "const", pwrites=[cm["t_const"]])
        gates = C.sb("gates", [128, NQ, 9], F32)
        t_gates = Tok()
        t_out = Tok()

        def do_xT(x_d, ntok):
            t_xT = Tok()
            with ExitStack() as e1:
                C1 = sub_ctx(C, e1)
                XTt = [C1.sb(f"XTt{i}", [128, 32, 512], BF16) for i in range(2)]
                xT_stage(C1, cm, x_d, ntok, xT_d, t_xT, D, XTt, [Tok(), Tok()])
                C.n = C1.n
            barrier(S)
            return t_xT

        if "nsa" in parts:
            t_xT = do_xT(x_nsa, SEQ)
            nsa_proj(C, cm, xT_d, t_xT, pos_nsa, w_nsa, gateb, sc, gates, t_gates, D, SEQ)
            nsa_attn(C, cm, sc, gates, t_gates, cst, cmp_pos, cmp_w1, cmp_w2, nsa_out, t_out, SEQ)
        if "swa" in parts:
            t_xT = do_xT(x_swa, N_SWA)
            swa_proj(C, cm, xT_d, t_xT, pos_swa, w_swa, sc2, D, N_SWA)
            swa_attn(C, cm, sc2, cst, sinks, mask0, swa_out, t_out, N_SWA)
        if "conv" in parts:
            t_xT = do_xT(x_conv, N_CONV)
            conv_phase(C, cm, xT_d, t_xT, w_conv, dwT, dwb, clng, clnb, yc_d, conv_out, t_out, D, N_CONV)
        print("mixer ops", S.nops, {k: len(v) for k, v in S.q.items()}, "sems", len(S.sem))
        S.emit()
    return nc


def _consts():
    c = {}
    c["ident"] = np.eye(128, dtype=np.float32)
    d = np.arange(128) % 64
    c["inv"] = np.power(np.float32(10000.0), -(2.0 * d).astype(np.float32) / np.float32(128.0)).astype(np.float32).reshape(128, 1)
    c["sgn"] = np.where(np.arange(128) < 64, -1.0, 1.0).astype(np.float32).reshape(128, 1)
    k = np.arange(SEQ)
    c["esel"] = (k[None, :] // 64 == np.arange(128)[:, None]).astype(np.float32)
    nn = np.arange(128)[:, None, None]
    dl = np.arange(17)[None, :, None]
    qq = (np.arange(384) % 128)[None, None, :]
    c["cmask"] = (16 * nn + 31 - 128 * dl <= qq).astype(np.float32)
    kk = np.arange(128)[:, None]
    q2 = (np.arange(384) % 128)[None, :]
    c["caus"] = (kk <= q2).astype(np.float32)
    c["upper"] = (kk > q2).astype(np.float32)
    n_cmp = 511
    cs = np.arange(512) * 16
    ss = np.arange(128) * 64
    ov = ((cs[:, None] < ss[None, :] + 64) & (cs[:, None] + 32 > ss[None, :])).astype(np.float32)
    ov[n_cmp:] = 0
    c["ovl"] = np.ascontiguousarray(ov.reshape(4, 128, 128).transpose(1, 0, 2))
    lowq = (np.arange(128) < 64)
    fp = np.zeros((128, 3), np.float32)
    fp[:, 0] = np.where(lowq, 1e4, 0.0)
    fp[:, 1] = np.where(lowq, 2e4, 1e4)
    fp[:, 2] = np.where(lowq, 0.0, 2e4)
    c["fpat"] = fp
    return c


def _swap(c0):
    return list(range(c0 + 64, c0 + 128)) + list(range(c0, c0 + 64))


def _rng(c0, n=128):
    return list(range(c0, c0 + n))


def mixer_inputs(x1, positions, p, l, cst):
    w_in = p["w_in"][l]
    maps = []
    for c in range(NCORES):
        b, h = c // 4, c % 4
        m = dict(cst)
        cols = []
        for g in range(3):
            c0 = 2048 + (3 * h + g) * 128
            cols += _rng(c0) + _swap(c0)
        for base in (3584, 4608, 5632):
            cols += _rng(base + h * 128) + _swap(base + h * 128)
        cols += _rng(4096 + h * 128)
        cols += _rng(5120 + h * 128) + _rng(6144 + h * 128) + _rng(6656 + 9 * h, 9)
        wn = np.zeros((D_MODEL, 1936), np.float32)
        wn[:, 0:1929] = w_in[:, cols]
        m["w_nsa"] = wn
        m["x_nsa"] = np.ascontiguousarray(x1[b])
        m["pos_nsa"] = np.ascontiguousarray(positions[b]).astype(np.int32)
        m["gateb"] = np.ascontiguousarray(p["nsa_gate_b"][l][9 * h:9 * h + 9])
        m["cmp_pos"] = np.ascontiguousarray(p["nsa_cmp_pos"][l])
        m["cmp_w1"] = np.ascontiguousarray(p["nsa_cmp_w1"][l])
        m["cmp_w2"] = np.ascontiguousarray(p["nsa_cmp_w2"][l])
        kvh, half = h // 2, h % 2
        cols = []
        for j in range(6):
            c0 = 6692 + (kvh * 6 + j) * 128
            cols += _rng(c0) + _swap(c0)
        cols += _rng(8228 + kvh * 128) + _swap(8228 + kvh * 128) + _rng(8484 + kvh * 128)
        m["w_swa"] = np.ascontiguousarray(w_in[:, cols])
        xs = np.zeros((N_SWA, D_MODEL), np.float32)
        ps_ = np.zeros((N_SWA,), np.int32)
        t0 = half * 4096
        if half > 0:
            xs[:] = x1[b, t0 - 128:t0 + 4096]
            ps_[:] = positions[b, t0 - 128:t0 + 4096]
        else:
            xs[128:] = x1[b, 0:4096]
            ps_[128:] = positions[b, 0:4096]
        m["x_swa"], m["pos_swa"] = xs, ps_
        m["sinks"] = np.ascontiguousarray(p["swa_sinks"][l][kvh * 6:kvh * 6 + 6])
        m["mask0"] = cst["upper"] if half > 0 else np.zeros_like(cst["upper"])
        xc = np.zeros((N_CONV, D_MODEL), np.float32)
        t0 = h * 2048
        if h > 0:
            xc[:] = x1[b, t0 - 128:t0 + 2048]
        else:
            xc[128:] = x1[b, 0:2048]
        m["x_conv"] = xc
        m["w_conv"] = np.ascontiguousarray(w_in[:, 0:2048])
        m["dwT"] = np.ascontiguousarray(p["conv_dw_w"][l].T)
        m["dwb"] = np.ascontiguousarray(p["conv_dw_b"][l])
        m["clng"] = np.ascontiguousarray(p["conv_ln_g"][l])
        m["clnb"] = np.ascontiguousarray(p["conv_ln_b"][l])
        maps.append(m)
    return maps


def assemble_mix(res):
    bf = res[0]["nsa_out"].dtype
    mix = np.zeros((2, SEQ, D_MODEL), dtype=bf)
    for c in range(NCORES):
        b, h = c // 4, c % 4
        mix[b, h * 2048:(h + 1) * 2048, 0:1024] = res[c]["conv_out"]
        mix[b, :, 1024 + h * 384:1024 + (h + 1) * 384] = res[c]["nsa_out"]
        kvh, half = h // 2, h % 2
        mix[b, half * 4096:(half + 1) * 4096, 2560 + kvh * 768:2560 + (kvh + 1) * 768] = res[c]["swa_out"]
    flat = mix.reshape(2 * SEQ, D_MODEL)
    return [np.ascontiguousarray(flat[c * NTC:(c + 1) * NTC].T) for c in range(NCORES)], mix


def _prog(name, fn):
    if name not in _PROGS:
        _PROGS[name] = fn()
    return _PROGS[name]


def _run(nc, maps):
    res = run_bass_kernel_spmd(nc, maps, core_ids=list(range(NCORES)))
    return res.results


def kernel(**inputs):
    p = {k: np.asarray(v) for k, v in inputs.items()}
    x = np.ascontiguousarray(p["x"], dtype=np.float32).reshape(2 * SEQ, D_MODEL)
    positions = p["positions"]
    cst = _consts()
    ident = cst["ident"]
    for l in range(2):
        nc = _prog("ffn", build_ffn_prog)
        maps = [{"x": np.ascontiguousarray(x[c * NTC:(c + 1) * NTC]), "wgu": p["ffn1_w_gu"][l], "wd": p["ffn1_w_down"][l], "g": p["ln1_g"][l], "b": p["ln1_b"][l], "ident": ident} for c in range(NCORES)]
        res = _run(nc, maps)
        x1 = np.concatenate([r["y"] for r in res], axis=0)
        nc = _prog("mixer", build_mixer_prog)
        res = _run(nc, mixer_inputs(x1.reshape(2, SEQ, D_MODEL), positions, p, l, cst))
        mixT, _ = assemble_mix(res)
        nc = _prog("c", build_c_prog)
        maps = [{"mixT": mixT[c], "x": np.ascontiguousarray(x1[c * NTC:(c + 1) * NTC]), "wo": p["w_out"][l], "g2": p["ln2_g"][l], "b2": p["ln2_b"][l],
                 "wgu": p["ffn2_w_gu"][l], "wd": p["ffn2_w_down"][l], "g": p["ln3_g"][l], "b": p["ln3_b"][l], "ident": ident} for c in range(NCORES)]
        res = _run(nc, maps)
        x = np.concatenate([r["y"] for r in res], axis=0)
    return x.reshape(2, SEQ, D_MODEL).astype(np.float32)
```

```python
import numpy as np
import concourse.bass as bass
import concourse.mybir as mybir
from contextlib import ExitStack

F32 = mybir.dt.float32
BF16 = mybir.dt.bfloat16
I32 = mybir.dt.int32
AF = mybir.ActivationFunctionType
ALU = mybir.AluOpType


class Tok:
    __slots__ = ("w", "r")

    def __init__(self):
        self.w = {}
        self.r = {}


class Sched:
    def __init__(self, nc, es):
        self.nc = nc
        self.es = es
        self.q = {e: [] for e in ("pe", "act", "dve", "pool", "sp")}
        self.sem = {}
        self.cnt = {}
        self.seen = {e: {} for e in self.q}
        self.nops = 0

    def _sem(self, name):
        if name not in self.sem:
            self.sem[name] = self.es.enter_context(self.nc.semaphore(name))
            self.cnt[name] = 0
        return name

    def op(self, e, fn, reads=(), writes=(), inc=True, dma=None, pwrites=()):
        deps = {}

        def need(ev):
            s, v = ev
            if deps.get(s, 0) < v:
                deps[s] = v

        for t in reads:
            for ev in t.w.items():
                need(ev)
        for t in writes:
            for ev in t.w.items():
                need(ev)
            for ev in t.r.items():
                need(ev)
        for t in pwrites:
            for ev in t.r.items():
                need(ev)
        own = "c_" + e
        for s, v in deps.items():
            if dma is None and s == own and e == "pe":
                continue
            if self.seen[e].get(s, 0) >= v:
                continue
            self.seen[e][s] = v
            self.q[e].append(("w", s, v))
        if dma is not None:
            s = self._sem("d_" + dma)
            self.cnt[s] += 16
            ev = (s, self.cnt[s])
            self.q[e].append(("o", fn, s, 16))
        else:
            s = self._sem(own)
            if inc:
                self.cnt[s] += 1
                ev = (s, self.cnt[s])
                self.q[e].append(("o", fn, s, 1))
            else:
                assert e == "pe"
                ev = (s, self.cnt[s] + 1)
                self.q[e].append(("o", fn, None, 0))
        for t in reads:
            if t.r.get(ev[0], 0) < ev[1]:
                t.r[ev[0]] = ev[1]
        for t in writes:
            t.w = {ev[0]: ev[1]}
            t.r = {}
        for t in pwrites:
            if t.w.get(ev[0], 0) < ev[1]:
                t.w[ev[0]] = ev[1]
            t.r = {}
        self.nops += 1

    def finish(self):
        for s, v in self.cnt.items():
            if v > 0 and self.seen["sp"].get(s, 0) < v:
                self.q["sp"].append(("w", s, v))

    def emit(self):
        nc = self.nc
        self.finish()
        with nc.Block() as block:

            def rep(e):
                def f(eng):
                    for it in self.q[e]:
                        if it[0] == "w":
                            eng.wait_ge(self.sem[it[1]], it[2])
                        else:
                            ins = it[1](eng)
                            if it[2] is not None:
                                ins.then_inc(self.sem[it[2]], it[3])

                return f

            block.tensor(rep("pe"))
            block.scalar(rep("act"))
            block.vector(rep("dve"))
            block.gpsimd(rep("pool"))
            block.sync(rep("sp"))


class Ctx:
    def __init__(self, nc, es):
        self.nc = nc
        self.es = es
        self.S = Sched(nc, es)
        self.n = 0

    def sb(self, name, shape, dt):
        self.n += 1
        return self.es.enter_context(self.nc.sbuf_tensor(f"{name}_{self.n}", shape, dt))

    def ps(self, name, shape, dt):
        self.n += 1
        return self.es.enter_context(self.nc.psum_tensor(f"{name}_{self.n}", shape, dt))


def mm(S, out, lhsT, rhs, start, stop, reads, writes, inc, **kw):
    S.op("pe", lambda e: e.matmul(out, lhsT, rhs, start=start, stop=stop, **kw), reads=reads, writes=writes, inc=inc)


def V(S, eng, method, *args, reads=(), writes=(), pwrites=(), **kw):
    S.op(eng, lambda e: getattr(e, method)(*args, **kw), reads=reads, writes=writes, pwrites=pwrites)


def DMA(S, q, out, in_, chan, reads=(), writes=(), pwrites=()):
    S.op(q, lambda e: e.dma_start(out=out, in_=in_), reads=reads, writes=writes, pwrites=pwrites, dma=chan)


def barrier(S):
    snap = dict(S.cnt)
    for e in S.q:
        for s, v in snap.items():
            if v > 0 and S.seen[e].get(s, 0) < v:
                S.seen[e][s] = v
                S.q[e].append(("w", s, v))


def build_xT(C, x_dram, r0, XT, t_XT, col0, ident, xrow, t_xrow, pst, t_pst, D, gcol=None, bcol=None, cnt=[0]):
    S = C.S
    DC = D // 128
    S.op("sp", lambda e: e.dma_start(out=xrow[:, 0:D], in_=x_dram[r0:r0 + 128, :]), writes=[t_xrow], dma="xrow")
    for g0 in range(0, DC, 4):
        k = cnt[0] % 2
        cnt[0] += 1
        n = min(4, DC - g0)
        for j in range(n):
            dc = g0 + j
            S.op("pe", (lambda dc=dc, j=j, k=k: lambda e: e.transpose(pst[k][:, j * 128:(j + 1) * 128], xrow[:, dc * 128:(dc + 1) * 128], ident[:]))(),
                 reads=[t_xrow], writes=[t_pst[k]], inc=(j == n - 1))
        src = pst[k][:, 0:n * 128].rearrange("p (a b) -> p a b", b=128)
        dst = XT[:, g0:g0 + n, col0:col0 + 128]
        if k == 0:
            S.op("dve", (lambda dst=dst, src=src: lambda e: e.tensor_copy(dst, src))(), reads=[t_pst[k]], writes=[t_XT])
        else:
            S.op("act", (lambda dst=dst, src=src: lambda e: e.copy(dst, src))(), reads=[t_pst[k]], writes=[t_XT])


def ln_rows(C, z_dram, zr0, t_z, y_dram, yr0, t_y, g_d, b_d, xrow, t_xrow, D, eps, tmp, chan="xrow"):
    S = C.S
    stats, mv, sd, rstd, gb, t_gb = tmp["stats"], tmp["mv"], tmp["sd"], tmp["rstd"], tmp["gb"], tmp["t_gb"]
    t_small = tmp["t_small"]
    NCH = D // 512 if D >= 512 else 1
    CW = min(512, D)
    S.op("sp", lambda e: e.dma_start(out=xrow[:, 0:D], in_=z_dram[zr0:zr0 + 128, :]), reads=[t_z], writes=[t_xrow], dma=chan)
    for k in range(NCH):
        S.op("dve", (lambda k=k: lambda e: e.bn_stats(stats[:, k, :], xrow[:, k * CW:(k + 1) * CW]))(), reads=[t_xrow], writes=[t_small])
    S.op("dve", lambda e: e.bn_aggr(mv[:, :], stats[:, 0:NCH, :]), reads=[t_small], writes=[t_small])
    S.op("act", lambda e: e.activation(sd[:, :], mv[:, 1:2], AF.Sqrt, bias=tmp["eps"][:, 0:1], scale=1.0), reads=[t_small], writes=[t_small])
    S.op("dve", lambda e: e.reciprocal(rstd[:, :], sd[:, :]), reads=[t_small], writes=[t_small])
    S.op("dve", lambda e: e.tensor_scalar(xrow[:, 0:D], xrow[:, 0:D], mv[:, 0:1], rstd[:, 0:1], ALU.subtract, ALU.mult), reads=[t_small, t_xrow], writes=[t_xrow])
    for k in range(NCH):
        i = tmp["gbi"][0] % 2
        tmp["gbi"][0] += 1
        S.op("sp", (lambda k=k, i=i: lambda e: e.dma_start(out=gb[i][:, 0, 0:CW], in_=g_d[k * CW:(k + 1) * CW].partition_broadcast(128)))(), writes=[t_gb[i]], dma=f"gb{i}")
        S.op("sp", (lambda k=k, i=i: lambda e: e.dma_start(out=gb[i][:, 1, 0:CW], in_=b_d[k * CW:(k + 1) * CW].partition_broadcast(128)))(), pwrites=[t_gb[i]], dma=f"gb{i}")
        sl = xrow[:, k * CW:(k + 1) * CW]
        S.op("pool", (lambda sl=sl, i=i: lambda e: e.tensor_tensor(sl, sl, gb[i][:, 0, 0:CW], ALU.mult))(), reads=[t_gb[i], t_xrow], writes=[t_xrow])
        S.op("pool", (lambda sl=sl, i=i: lambda e: e.tensor_tensor(sl, sl, gb[i][:, 1, 0:CW], ALU.add))(), reads=[t_gb[i], t_xrow], writes=[t_xrow])
    S.op("sp", lambda e: e.dma_start(out=y_dram[yr0:yr0 + 128, :], in_=xrow[:, 0:D]), reads=[t_xrow], pwrites=[t_y], dma=chan + "_st")


def alloc_common(C, D):
    nc = C.nc
    cm = {}
    cm["ident"] = C.sb("ident", [128, 128], F32)
    cm["identb"] = C.sb("identb", [128, 128], BF16)
    cm["t_const"] = Tok()
    cm["xrow"] = C.sb("xrow", [128, D], F32)
    cm["t_xrow"] = Tok()
    cm["ps"] = [C.ps(f"ps{i}", [128, 512], F32) for i in range(8)]
    cm["t_ps"] = [Tok() for _ in range(8)]
    tmp = {
        "stats": C.sb("stats", [128, 8, 6], F32), "mv": C.sb("mv", [128, 2], F32), "sd": C.sb("sd", [128, 1], F32),
        "rstd": C.sb("rstd", [128, 1], F32), "gb": [C.sb(f"gb{i}", [128, 2, 512], F32) for i in range(2)],
        "t_gb": [Tok(), Tok()], "t_small": Tok(), "gbi": [0], "eps": C.sb("epsc", [128, 1], F32),
    }
    cm["tmp"] = tmp
    return cm


def init_common(C, cm, ident_d):
    S = C.S
    S.op("sp", lambda e: e.dma_start(out=cm["ident"][:, :], in_=ident_d[:, :]), writes=[cm["t_const"]], dma="const")
    S.op("dve", lambda e: e.tensor_copy(cm["identb"][:, :], cm["ident"][:, :]), reads=[cm["t_const"]], writes=[cm["t_const"]])
    S.op("dve", lambda e: e.memset(cm["tmp"]["eps"][:, :], 1e-5), writes=[cm["tmp"]["t_small"]])


def down_ln(C, cm, HT, t_HT, FC, wd_v, Wb, t_W, load_w, wi, xi, x_dram, t_x, z_dram, t_zrows, y_dram, t_y, g_d, b_d, xres, t_xres, zt, t_zt, r0, T, D, alpha, eps, lnrow=None):
    S = C.S
    ps, t_ps = cm["ps"], cm["t_ps"]
    xrow, t_xrow = cm["xrow"], cm["t_xrow"]
    lnchan = "xrow"
    if lnrow is not None:
        xrow, t_xrow = lnrow
        lnchan = "zrow"
    TS = T // 128
    OW = min(512, D)
    OT = D // OW
    FG = 8
    NG = (FC + FG - 1) // FG
    for ot in range(OT):
        for fg in range(NG):
            idx = wi[0] % 2
            wi[0] += 1
            nf = min(FG, FC - fg * FG)
            Wv = Wb[idx][:, 0:FG * OW].rearrange("p (c n) -> p c n", n=OW)
            load_w(Wv[:, 0:nf, :], wd_v[:, fg * FG:fg * FG + nf, ot * OW:(ot + 1) * OW], idx)
            for ts in range(TS):
                for j in range(nf):
                    fc = fg * FG + j
                    mm(S, ps[ts][:, 0:OW], HT[:, fc, ts * 128:(ts + 1) * 128], Wv[:, j, :], fc == 0, fc == FC - 1,
                       reads=[t_W[idx], t_HT], writes=[t_ps[ts]], inc=(fc == FC - 1 or j == nf - 1))
        for ts in range(TS):
            i = xi[0] % 2
            xi[0] += 1
            rr = r0 + ts * 128
            S.op("sp", (lambda i=i, rr=rr, ot=ot: lambda e: e.dma_start(out=xres[i][:, :], in_=x_dram[rr:rr + 128, ot * OW:(ot + 1) * OW]))(),
                 reads=[t_x], writes=[t_xres[i]], dma=f"xres{i}")
            S.op("dve", (lambda i=i, ts=ts: lambda e: e.scalar_tensor_tensor(zt[i][:, :], xres[i][:, :], float(alpha), ps[ts][:, 0:OW], ALU.mult, ALU.add))(),
                 reads=[t_xres[i], t_ps[ts]], writes=[t_zt[i]])
            S.op("sp", (lambda i=i, rr=rr, ot=ot: lambda e: e.dma_start(out=z_dram[rr:rr + 128, ot * OW:(ot + 1) * OW], in_=zt[i][:, :]))(),
                 reads=[t_zt[i]], pwrites=[t_zrows[rr // 128]], dma=f"zt{i}")
    for ts in range(TS):
        rr = r0 + ts * 128
        ln_rows(C, z_dram, rr, t_zrows[rr // 128], y_dram, rr, t_y, g_d, b_d, xrow, t_xrow, D, eps, cm["tmp"], chan=lnchan)


def ffn_stage(C, cm, x_dram, wgu, wd, g_d, b_d, z_dram, y_dram, NT, D, F, alpha, eps, t_x=None, t_z=None, t_y=None, cast_dma=True):
    S = C.S
    DC, FC = D // 128, F // 128
    T = min(512, NT)
    NTT = NT // T
    TS = T // 128
    OW = min(512, D)
    OT = D // OW
    FG = 8
    ident = cm["ident"]
    xrow, t_xrow = cm["xrow"], cm["t_xrow"]
    ps, t_ps = cm["ps"], cm["t_ps"]
    XT = C.sb("XT", [128, DC, T], BF16)
    t_XT = Tok()
    HT = C.sb("HT", [128, FC, T], BF16)
    t_HT = Tok()
    zrow = C.sb("zrow", [128, D], F32)
    t_zrow = Tok()
    WSZ = max(DC * 256, FG * OW)
    Wb = [C.sb(f"W{i}", [128, WSZ], BF16) for i in range(2)]
    t_W = [Tok(), Tok()]
    if not cast_dma:
        Wst = [C.sb(f"Wst{i}", [128, WSZ], F32) for i in range(2)]
        t_Wst = [Tok(), Tok()]
    sg = [C.sb(f"sg{i}", [128, T], F32) for i in range(4)]
    t_sg = [Tok() for _ in range(4)]
    xres = [C.sb(f"xres{i}", [128, OW], F32) for i in range(2)]
    t_xres = [Tok(), Tok()]
    zt = [C.sb(f"zt{i}", [128, OW], F32) for i in range(2)]
    t_zt = [Tok(), Tok()]
    t_x = t_x or Tok()
    t_zrows = [Tok() for _ in range(NT // 128)]
    t_y = t_y or Tok()
    wgu_v = wgu.rearrange("(c p) n -> p c n", p=128)
    wd_v = wd.rearrange("(c p) n -> p c n", p=128)
    wi = [0]
    xi = [0]

    def load_w(dst3, src3, idx):
        if cast_dma:
            S.op("pool", lambda e: e.dma_start(out=dst3, in_=src3), writes=[t_W[idx]], dma=f"w{idx}")
        else:
            a, b = dst3.shape[1], dst3.shape[2]
            st = Wst[idx][:, 0:a * b].rearrange("p (a b) -> p a b", b=b)
            S.op("sp", lambda e: e.dma_start(out=st, in_=src3), writes=[t_Wst[idx]], dma=f"w{idx}")
            S.op("pool", lambda e: e.tensor_copy(dst3, st), reads=[t_Wst[idx]], writes=[t_W[idx]])

    for tt in range(NTT):
        r0 = tt * T
        for ts in range(TS):
            build_xT(C, x_dram, r0 + ts * 128, XT, t_XT, ts * 128, ident, xrow, t_xrow, ps[4:6], t_ps[4:6], D)
        assert FC % 2 == 0
        for fp in range(FC // 2):
            for kind in range(2):
                idx = wi[0] % 2
                wi[0] += 1
                Wv = Wb[idx][:, 0:DC * 256].rearrange("p (c n) -> p c n", n=256)
                load_w(Wv, wgu_v[:, :, kind * F + fp * 256:kind * F + (fp + 1) * 256], idx)
                for h in range(2):
                    fc = 2 * fp + h
                    bank = kind * 2 + h
                    for dc in range(DC):
                        mm(S, ps[bank][:, 0:T], Wv[:, dc, h * 128:(h + 1) * 128], XT[:, dc, :], dc == 0, dc == DC - 1,
                           reads=[t_W[idx], t_XT], writes=[t_ps[bank]], inc=(dc == DC - 1))
                    si = (fp % 2) * 2 + h
                    if kind == 0:
                        V(S, "act", "activation", sg[si][:, 0:T], ps[bank][:, 0:T], AF.Silu, reads=[t_ps[bank]], writes=[t_sg[si]])
                    else:
                        V(S, "dve", "scalar_tensor_tensor", HT[:, fc, :], sg[si][:, 0:T], 0.5, ps[bank][:, 0:T], ALU.mult, ALU.mult,
                          reads=[t_sg[si], t_ps[bank]], pwrites=[t_HT] if fc > 0 else (), writes=[t_HT] if fc == 0 else ())
        down_ln(C, cm, HT, t_HT, FC, wd_v, Wb, t_W, load_w, wi, xi, x_dram, t_x, z_dram, t_zrows, y_dram, t_y, g_d, b_d, xres, t_xres, zt, t_zt, r0, T, D, alpha, eps, lnrow=(zrow, t_zrow))
    return t_y


def sub_ctx(C, es):
    C2 = Ctx.__new__(Ctx)
    C2.nc, C2.es, C2.S, C2.n = C.nc, es, C.S, C.n + 1000
    return C2


def outproj_stage(C, cm, mixT_d, x_dram, wo, g_d, b_d, z_dram, y_dram, NT, D, alpha, eps):
    S = C.S
    DC = D // 128
    T = min(512, NT)
    OW = min(512, D)
    MT = [C.sb(f"MT{i}", [128, DC, T], BF16) for i in range(2)]
    t_MT = [Tok(), Tok()]
    Wb = [C.sb(f"Wo{i}", [128, 8 * OW], BF16) for i in range(2)]
    t_W = [Tok(), Tok()]
    xres = [C.sb(f"xres{i}", [128, OW], F32) for i in range(2)]
    t_xres = [Tok(), Tok()]
    zt = [C.sb(f"zt{i}", [128, OW], F32) for i in range(2)]
    t_zt = [Tok(), Tok()]
    t_zrows = [Tok() for _ in range(NT // 128)]
    t_y = Tok()
    t_x = Tok()
    wo_v = wo.rearrange("(c p) n -> p c n", p=128)
    mT_v = mixT_d.rearrange("(c p) t -> p c t", p=128)
    wi = [0]
    xi = [0]

    def load_w(dst3, src3, idx):
        S.op("pool", lambda e: e.dma_start(out=dst3, in_=src3), writes=[t_W[idx]], dma=f"wo{idx}")

    for tt in range(NT // T):
        k = tt % 2
        r0 = tt * T
        DMA(S, "sp", MT[k][:, :, :], mT_v[:, :, r0:r0 + T], f"mt{k}", writes=[t_MT[k]])
        down_ln(C, cm, MT[k], t_MT[k], DC, wo_v, Wb, t_W, load_w, wi, xi, x_dram, t_x, z_dram, t_zrows, y_dram, t_y, g_d, b_d, xres, t_xres, zt, t_zt, r0, T, D, alpha, eps)
    return t_y

import math

PI = math.pi
TWO_PI = 2 * math.pi
C1 = 6.28125
C2 = TWO_PI - C1
SCALE = 128 ** -0.5
NEGM = -30000.0
DBG = {"level": 9, "nq": None}


def xT_stage(C, cm, x_dram, ntok, xT_d, t_xT_d, D, XTt, t_XTt):
    S = C.S
    r = 0
    k = 0
    while r < ntok:
        T = min(512, ntok - r)
        for ts in range(T // 128):
            build_xT(C, x_dram, r + ts * 128, XTt[k], t_XTt[k], ts * 128, cm["ident"], cm["xrow"], cm["t_xrow"], cm["ps"][4:6], cm["t_ps"][4:6], D)
        DMA(S, "sp", xT_d[:, :, r:r + T], XTt[k][:, :, 0:T], f"xtst{k}", reads=[t_XTt[k]], pwrites=[t_xT_d])
        k ^= 1
        r += T


def alloc_rope(C):
    rt = {}
    for n in ("ang", "tmpf", "r", "r2", "m", "St", "Ct"):
        rt[n] = C.sb("rt_" + n, [128, 512], F32)
    rt["posi"] = C.sb("rt_posi", [128, 512], I32)
    rt["ki"] = C.sb("rt_ki", [128, 512], I32)
    rt["t"] = Tok()
    rt["t_tab"] = Tok()
    return rt


def rope_tables(C, cm, rt, pos_d, t0, T):
    S = C.S
    inv, sgn = cm["inv"], cm["sgn"]
    t, tt = rt["t"], rt["t_tab"]
    ang, tmpf, r, r2, m, St, Ct, posi, ki = (rt[n][:, 0:T] for n in ("ang", "tmpf", "r", "r2", "m", "St", "Ct", "posi", "ki"))
    DMA(S, "sp", posi, pos_d[t0:t0 + T].partition_broadcast(128), "posi", writes=[t])
    V(S, "dve", "tensor_copy", ang, posi, reads=[t], writes=[t])
    V(S, "dve", "tensor_scalar", ang, ang, inv[:, 0:1], None, ALU.mult, reads=[t, cm["t_const"]], writes=[t])
    V(S, "dve", "tensor_scalar", tmpf, ang, 1.0 / TWO_PI, None, ALU.mult, reads=[t], writes=[t])
    V(S, "dve", "tensor_copy", ki, tmpf, reads=[t], writes=[t])
    V(S, "dve", "tensor_copy", tmpf, ki, reads=[t], writes=[t])
    V(S, "dve", "scalar_tensor_tensor", r, tmpf, -C1, ang, ALU.mult, ALU.add, reads=[t], writes=[t])
    V(S, "dve", "scalar_tensor_tensor", r, tmpf, -C2, r, ALU.mult, ALU.add, reads=[t], writes=[t])
    V(S, "dve", "tensor_scalar", m, r, PI, None, ALU.is_gt, reads=[t], writes=[t])
    V(S, "dve", "scalar_tensor_tensor", r, m, -TWO_PI, r, ALU.mult, ALU.add, reads=[t], writes=[t])
    V(S, "dve", "tensor_scalar", m, r, -PI, None, ALU.is_lt, reads=[t], writes=[t])
    V(S, "dve", "scalar_tensor_tensor", r, m, TWO_PI, r, ALU.mult, ALU.add, reads=[t], writes=[t])
    V(S, "dve", "tensor_scalar", r2, r, PI / 2, None, ALU.add, reads=[t], writes=[t])
    V(S, "dve", "tensor_scalar", m, r2, PI, None, ALU.is_gt, reads=[t], writes=[t])
    V(S, "dve", "scalar_tensor_tensor", r2, m, -TWO_PI, r2, ALU.mult, ALU.add, reads=[t], writes=[t])
    V(S, "dve", "tensor_scalar", r, r, 3.14159, -3.14159, ALU.min, ALU.max, reads=[t], writes=[t])
    V(S, "dve", "tensor_scalar", r2, r2, 3.14159, -3.14159, ALU.min, ALU.max, reads=[t], writes=[t])
    V(S, "act", "activation", St, r, AF.Sin, scale=sgn[:, 0:1], reads=[t, cm["t_const"]], writes=[tt])
    V(S, "act", "activation", Ct, r2, AF.Sin, reads=[t], pwrites=[tt])


class Proj:
    def __init__(self, C, cm, D, XTt, t_XTt, Wg, t_Wg):
        self.C, self.cm, self.D = C, cm, D
        self.S = C.S
        self.DC = D // 128
        self.XTt, self.t_XTt, self.Wg, self.t_Wg = XTt, t_XTt, Wg, t_Wg
        self.xi = 0
        self.wi = 0
        self.pi = 0
        self.stg = [C.sb(f"pstg{i}", [128, 512], BF16) for i in range(2)]
        self.t_stg = [Tok(), Tok()]
        self.t1 = [C.sb(f"pt1_{i}", [128, 512], F32) for i in range(2)]
        self.t_t1 = [Tok(), Tok()]
        self.si = 0

    def load_w(self, w_d, c0, ncols):
        S = self.S
        i = self.wi % 2
        self.wi += 1
        wv = w_d.rearrange("(c p) n -> p c n", p=128)
        npad = (ncols + 127) // 128 * 128
        Wv = self.Wg[i][:, 0:self.DC * npad].rearrange("p (c n) -> p c n", n=npad)[:, :, 0:ncols]
        DMA(S, "pool", Wv, wv[:, :, c0:c0 + ncols], f"wg{i}", writes=[self.t_Wg[i]])
        return Wv, self.t_Wg[i]

    def load_x(self, xT_d, t_xT_d, r, T):
        S = self.S
        i = self.xi % 2
        self.xi += 1
        DMA(S, "sp", self.XTt[i][:, :, 0:T], xT_d[:, :, r:r + T], f"xtl{i}", reads=[t_xT_d], writes=[self.t_XTt[i]])
        return self.XTt[i], self.t_XTt[i]

    def fm(self, Wv, tW, c0, X, tX, T, bank):
        S = self.S
        ps, t_ps = self.cm["ps"], self.cm["t_ps"]
        for dc in range(self.DC):
            mm(S, ps[bank][:, 0:T], Wv[:, dc, c0:c0 + 128], X[:, dc, 0:T], dc == 0, dc == self.DC - 1, reads=[tW, tX], writes=[t_ps[bank]], inc=(dc == self.DC - 1))
        return ps[bank][:, 0:T], t_ps[bank]

    def tm(self, Wv, tW, c0, ncols, X, tX, ts, bank):
        S = self.S
        ps, t_ps = self.cm["ps"], self.cm["t_ps"]
        for dc in range(self.DC):
            mm(S, ps[bank][:, 0:ncols], X[:, dc, ts * 128:(ts + 1) * 128], Wv[:, dc, c0:c0 + ncols], dc == 0, dc == self.DC - 1, reads=[tW, tX], writes=[t_ps[bank]], inc=(dc == self.DC - 1))
        return ps[bank][:, 0:ncols], t_ps[bank]

    def rope_pair(self, Wv, tW, c0, X, tX, T, rt, dst_ap, t_dst):
        S = self.S
        b = (self.pi % 2) * 2
        self.pi += 1
        A, tA = self.fm(Wv, tW, c0, X, tX, T, b)
        B, tB = self.fm(Wv, tW, c0 + 128, X, tX, T, b + 1)
        k = self.si % 2
        self.si += 1
        t1, tt1 = self.t1[k][:, 0:T], self.t_t1[k]
        sg, tsg = self.stg[k][:, 0:T], self.t_stg[k]
        V(S, "dve", "tensor_tensor", t1, A, rt["Ct"][:, 0:T], ALU.mult, reads=[tA, rt["t_tab"]], writes=[tt1])
        V(S, "dve", "tensor_tensor", sg, B, rt["St"][:, 0:T], ALU.mult, reads=[tB, rt["t_tab"]], writes=[tsg])
        V(S, "pool", "tensor_tensor", sg, sg, t1, ALU.add, reads=[tt1, tsg], writes=[tsg])
        DMA(S, "sp", dst_ap, sg if dst_ap.ndim == 2 else sg.rearrange("p (a b) -> p a b", b=128), f"pstg{k}", reads=[tsg], pwrites=[t_dst])

    def fm_plain(self, Wv, tW, c0, X, tX, T, dst_ap, t_dst):
        S = self.S
        b = (self.pi % 2) * 2
        self.pi += 1
        A, tA = self.fm(Wv, tW, c0, X, tX, T, b)
        k = self.si % 2
        self.si += 1
        sg, tsg = self.stg[k][:, 0:T], self.t_stg[k]
        V(S, "act", "copy", sg, A, reads=[tA], writes=[tsg])
        DMA(S, "sp", dst_ap, sg, f"pstg{k}", reads=[tsg], pwrites=[t_dst])


def tiles_of(ntok, first):
    out = []
    r = 0
    if first:
        out.append((0, first))
        r = first
    while r < ntok:
        out.append((r, min(512, ntok - r)))
        r += 512
    return out


class Attn:
    def __init__(self, C, cm):
        self.C, self.cm, self.S = C, cm, C.S
        self.Pt = [C.sb(f"Pt{i}", [128, 384], BF16) for i in range(3)]
        self.t_Pt = [Tok() for _ in range(3)]
        self.pi = 0
        self.si = 0

    def step(self, Klhs, tK, Qt, tQ, extra, mulmask, tmask, pv, first):
        S, cm = self.S, self.cm
        ps, t_ps = cm["ps"], cm["t_ps"]
        b = self.si % 2
        self.si += 1
        mm(S, ps[b][:, 0:384], Klhs, Qt, True, extra is None, reads=[tK, tQ], writes=[t_ps[b]], inc=(extra is None))
        if extra is not None:
            mm(S, ps[b][:, 0:384], extra[0], extra[1], False, True, reads=[extra[2]], writes=[t_ps[b]], inc=True)
        i = self.pi % 3
        self.pi += 1
        P, tP = self.Pt[i], self.t_Pt[i]
        V(S, "act", "activation", P[:, :], ps[b][:, 0:384], AF.Exp, scale=SCALE, reads=[t_ps[b]], writes=[tP])
        if mulmask is not None:
            V(S, "pool", "tensor_tensor", P[:, :], P[:, :], mulmask, ALU.mult, reads=[tP, tmask], writes=[tP])
        for (bank, width, rhs, t_rhs) in pv:
            for g in range(3):
                st = first and g == 0
                mm(S, ps[bank][:, g * width:(g + 1) * width], P[:, g * 128:(g + 1) * 128], rhs, st, False, reads=[tP, t_rhs], writes=[t_ps[bank]], inc=(g == 2),
                   skip_group_check=True)


def nsa_proj(C, cm, xT_d, t_xT_d, pos_d, w_d, gateb_d, sc, gates, t_gates, D, NTOK):
    S = C.S
    with ExitStack() as es:
        C2 = Ctx.__new__(Ctx)
        C2.nc, C2.es, C2.S, C2.n = C.nc, es, C.S, C.n + 1000
        DC = D // 128
        XTt = [C2.sb(f"XTt{i}", [128, DC, 512], BF16) for i in range(2)]
        Wg = [C2.sb(f"Wg{i}", [128, DC * 512], BF16) for i in range(2)]
        P = Proj(C2, cm, D, XTt, [Tok(), Tok()], Wg, [Tok(), Tok()])
        rt = alloc_rope(C2)
        gb = C2.sb("gateb", [128, 9], F32)
        t_gb = Tok()
        DMA(S, "sp", gb[:, :], gateb_d[0:9].partition_broadcast(128), "gateb", writes=[t_gb])
        vst_ = [C2.sb(f"vst{i}", [128, 2, 132], BF16) for i in range(2)]
        vst = [v_[:, :, 0:129] for v_ in vst_]
        t_vst = [Tok(), Tok()]
        for i in range(2):
            V(S, "pool", "memset", vst[i][:, :, 128:129], 1.0, writes=[t_vst[i]])
        gtmp = C2.sb("gtmp", [128, 9], F32)
        t_gtmp = Tok()
        tiles = tiles_of(NTOK, 0)
        t_sc = sc["t"]
        groups = [
            (0, 512, [("q", 0, 0), ("q", 256, 1)]),
            (512, 512, [("q", 0, 2), ("k", 256, "kcT")]),
            (1024, 512, [("k", 0, "kslT"), ("k", 256, "kwT")]),
        ]
        for (c0, ncols, units) in groups:
            Wv, tW = P.load_w(w_d, c0, ncols)
            for (r, T) in tiles:
                X, tX = P.load_x(xT_d, t_xT_d, r, T)
                rope_tables(C2, cm, rt, pos_d, r, T)
                for (kind, off, dst) in units:
                    if kind == "q":
                        dap = sc["QT"][:, r // 128:(r + T) // 128, dst, :]
                    else:
                        dap = sc[dst][:, r:r + T]
                    P.rope_pair(Wv, tW, off, X, tX, T, rt, dap, t_sc)
        Wv, tW = P.load_w(w_d, 1536, 400)
        vi = 0
        for (r, T) in (tiles if DBG["level"] >= -1 else ()):
            X, tX = P.load_x(xT_d, t_xT_d, r, T)
            P.fm_plain(Wv, tW, 0, X, tX, T, sc["vcT"][:, r:r + T], t_sc)
            for ts in range(T // 128):
                bank = 4 + (vi % 2)
                k = vi % 2
                vi += 1
                A, tA = P.tm(Wv, tW, 128, 272, X, tX, ts, bank)
                qi = r // 128 + ts
                V(S, "act", "copy", vst[k][:, :, 0:128], A[:, 0:256].rearrange("p (a b) -> p a b", b=128), reads=[tA], pwrites=[t_vst[k]])
                DMA(S, "sp", sc["vsl"][:, qi, :], vst[k][:, 0, :], f"vst{k}", reads=[t_vst[k]], pwrites=[t_sc])
                DMA(S, "sp", sc["vw"][:, qi, :], vst[k][:, 1, :], f"vst{k}", reads=[t_vst[k]], pwrites=[t_sc])
                V(S, "dve", "tensor_tensor", gtmp[:, :], A[:, 256:265], gb[:, :], ALU.add, reads=[tA, t_gb, t_vst[k]], writes=[t_gtmp])
                V(S, "act", "activation", gates[:, qi, :], gtmp[:, :], AF.Sigmoid, reads=[t_gtmp], pwrites=[t_gates])
        C.n = C2.n
    barrier(S)


def load_cast(S, dst, src, chan, t):
    DMA(S, "pool", dst, src, chan, pwrites=[t])


def nsa_attn(C, cm, sc, gates, t_gates, cst, cmp_pos_d, cmp_w1_d, cmp_w2_d, out_d, t_out, NTOK):
    S = C.S
    NQ = NTOK // 128
    ps, t_ps = cm["ps"], cm["t_ps"]
    if DBG["level"] < 0:
        return
    with ExitStack() as es:
        C2 = Ctx.__new__(Ctx)
        C2.nc, C2.es, C2.S, C2.n = C.nc, es, C.S, C.n + 1000
        t_sc = sc["t"]
        tK = Tok()
        kslT = C2.sb("kslT", [128, NTOK], BF16)
        kwT = C2.sb("kwT", [128, NTOK], BF16)
        kcT = C2.sb("kcT", [128, NTOK], BF16)
        vcT = C2.sb("vcT", [128, NTOK], BF16)
        vsl = C2.sb("vsl", [128, NQ, 129], BF16)
        vw = C2.sb("vw", [128, NQ, 129], BF16)
        for (dst, src, ch) in ((kslT, sc["kslT"], "l0"), (kwT, sc["kwT"], "l1"), (kcT, sc["kcT"], "l2"), (vcT, sc["vcT"], "l3")):
            DMA(S, "sp", dst[:, :], src[:, :], ch, reads=[t_sc], pwrites=[tK])
        DMA(S, "sp", vsl[:, :, :], sc["vsl"][:, :, :], "l4", reads=[t_sc], pwrites=[tK])
        DMA(S, "sp", vw[:, :, :], sc["vw"][:, :, :], "l5", reads=[t_sc], pwrites=[tK])
        tc_ = Tok()
        esel = C2.sb("esel", [128, NTOK], BF16)
        cmask = C2.sb("cmask", [128, 17, 384], BF16)
        caus = C2.sb("caus", [128, 384], BF16)
        upper = C2.sb("upper", [128, 384], BF16)
        ovl = C2.sb("ovl", [128, 4, 128], BF16)
        fpat = C2.sb("fpat", [128, 3], F32)
        load_cast(S, esel[:, :], cst["esel"][:, :], "c0", tc_)
        load_cast(S, cmask[:, :, :], cst["cmask"][:, :, :], "c1", tc_)
        load_cast(S, caus[:, :], cst["caus"][:, :], "c2", tc_)
        load_cast(S, upper[:, :], cst["upper"][:, :], "c3", tc_)
        load_cast(S, ovl[:, :, :], cst["ovl"][:, :, :], "c4", tc_)
        DMA(S, "sp", fpat[:, :], cst["fpat"][:, :], "c5", pwrites=[tc_])
        kcmpT = C2.sb("kcmpT", [128, 512], BF16)
        vcaug = C2.sb("vcaug", [128, 4, 129], BF16)
        t_cmp = Tok()
        W1 = C2.sb("W1", [128, 32, 128], BF16)
        W2 = C2.sb("W2", [128, 128], BF16)
        posT = C2.sb("posT", [128, 32], BF16)
        posN = C2.sb("posN", [32, 128], F32)
        t_pn = Tok()
        biasc = C2.sb("biasc", [128, 1], F32)
        Acm = C2.sb("Acm", [128, 512], BF16)
        t_w = Tok()
        t_A = Tok()
        V(S, "pool", "memset", vcaug[:, :, 128:129], 1.0, pwrites=[t_cmp])
        for which, src in (((0, kcT), (1, vcT)) if DBG["level"] >= 1 else ()):
            S.op("pool", (lambda which=which: lambda e: e.dma_start(out=W1[:, :, :], in_=cmp_w1_d[which].rearrange("(l d) h -> d l h", d=128)))(), writes=[t_w], dma="cw1")
            S.op("pool", (lambda which=which: lambda e: e.dma_start(out=W2[:, :], in_=cmp_w2_d[which]))(), pwrites=[t_w], dma="cw2")
            S.op("sp", (lambda which=which: lambda e: e.dma_start(out=posN[0:32, :], in_=cmp_pos_d[which]))(), writes=[t_pn], dma="cw3")
            S.op("pe", lambda e: e.transpose(ps[6][:, 0:32], posN[0:32, :], cm["ident"][0:32, 0:32]), reads=[t_pn, cm["t_const"]], writes=[t_ps[6]])
            V(S, "dve", "tensor_copy", posT[:, :], ps[6][:, 0:32], reads=[t_ps[6]], pwrites=[t_w])
            for l in range(32):
                mm(S, ps[6][:, 0:1], W1[:, l, :], posT[:, l:l + 1], l == 0, l == 31, reads=[t_w], writes=[t_ps[6]], inc=(l == 31))
            V(S, "dve", "tensor_copy", biasc[:, :], ps[6][:, 0:1], reads=[t_ps[6]], writes=[t_A])
            for l in range(32):
                mm(S, ps[7][:, 0:511], W1[:, l, :], src[:, l:l + 16 * 510 + 1:16], l == 0, l == 31, reads=[t_w, tK], writes=[t_ps[7]], inc=(l == 31))
            V(S, "pool", "memset", Acm[:, 511:512], 0.0, writes=[t_A])
            V(S, "act", "activation", Acm[:, 0:511], ps[7][:, 0:511], AF.Silu, bias=biasc[:, 0:1], reads=[t_ps[7], t_A], writes=[t_A])
            if which == 0:
                mm(S, ps[6][:, 0:512], W2[:, :], Acm[:, :], True, True, reads=[t_w, t_A], writes=[t_ps[6]], inc=True)
                V(S, "dve", "tensor_copy", kcmpT[:, :], ps[6][:, 0:512], reads=[t_ps[6]], pwrites=[t_cmp])
            else:
                for m in range(4):
                    mm(S, ps[6][:, m * 128:(m + 1) * 128], Acm[:, m * 128:(m + 1) * 128], W2[:, :], True, True, reads=[t_w, t_A], writes=[t_ps[6]], inc=(m == 3))
                V(S, "dve", "tensor_copy", vcaug[:, :, 0:128], ps[6][:, 0:512].rearrange("p (a b) -> p a b", b=128), reads=[t_ps[6]], pwrites=[t_cmp])
        AT = Attn(C2, cm)
        Qt = [C2.sb(f"Qt{i}", [128, 384], BF16) for i in range(2)]
        t_Qt = [Tok(), Tok()]
        imp = C2.sb("imp", [128, 128], F32)
        imp2 = C2.sb("imp2", [128, 128], F32)
        mx = C2.sb("mx", [128, 8], F32)
        mx2 = C2.sb("mx2", [128, 8], F32)
        mneg = C2.sb("mneg", [128, 128], F32)
        mnegT = C2.sb("mnegT", [128, 384], BF16)
        den = C2.sb("den", [128, 9], F32)
        coef = C2.sb("coef", [128, 9], F32)
        rdc = C2.sb("rdc", [128, 3], F32)
        ot = [C2.sb(f"ot{i}", [128, 384], F32) for i in range(2)]
        ob = [C2.sb(f"ob{i}", [128, 384], BF16) for i in range(2)]
        t_ot = [Tok(), Tok()]
        t_i = Tok()
        t_mT = Tok()
        t_d = Tok()
        for i in range(NQ if DBG["nq"] is None else DBG["nq"]):
            if DBG["level"] < 2:
                break
            q = i % 2
            DMA(S, "sp", Qt[q][:, :], sc["QT"][:, i, :, :].rearrange("p a b -> p (a b)"), f"qt{q}", reads=[t_sc], writes=[t_Qt[q]])
            Q, tQ = Qt[q][:, :], t_Qt[q]
            mlist = [m for m in range(4) if 128 * m <= 8 * i + 6]
            for n_, m in enumerate(mlist):
                dlt = i - 16 * m
                mask = cmask[:, dlt, :] if dlt <= 16 else None
                AT.step(kcmpT[:, m * 128:(m + 1) * 128], t_cmp, Q, tQ, None, mask, tc_,
                        [(2, 129, vcaug[:, m, :], t_cmp), (3, 128, ovl[:, m, :], tc_)], n_ == 0)
            if DBG["level"] < 3:
                continue
            V(S, "dve", "tensor_scalar", rdc[:, :], ps[2][:, 0:387].rearrange("p (g c) -> p g c", c=129)[:, :, 128], 1e-30, None, ALU.max, reads=[t_ps[2]], writes=[t_d])
            V(S, "dve", "reciprocal", rdc[:, :], rdc[:, :], reads=[t_d], writes=[t_d])
            V(S, "dve", "tensor_scalar", imp[:, :], ps[3][:, 0:128], rdc[:, 0:1], None, ALU.mult, reads=[t_ps[3], t_d], writes=[t_i])
            for g in (1, 2):
                V(S, "dve", "scalar_tensor_tensor", imp[:, :], ps[3][:, g * 128:(g + 1) * 128], rdc[:, g:g + 1], imp[:, :], ALU.mult, ALU.add, reads=[t_ps[3], t_d, t_i], writes=[t_i])
            lo = max(0, 2 * i - 1)
            hi = min(128, 2 * i + 2)
            V(S, "dve", "tensor_tensor", imp[:, lo:hi], imp[:, lo:hi], fpat[:, lo - (2 * i - 1):hi - (2 * i - 1)], ALU.add, reads=[t_i, tc_], writes=[t_i])
            V(S, "dve", "memset", imp[:, 0:1], 3e4, writes=[t_i])
            V(S, "dve", "max", mx[:, :], imp[:, :], reads=[t_i], writes=[t_d])
            V(S, "dve", "match_replace", imp2[:, :], mx[:, :], imp[:, :], -1e9, reads=[t_i, t_d], writes=[t_d])
            V(S, "dve", "max", mx2[:, :], imp2[:, :], reads=[t_d], writes=[t_d])
            V(S, "dve", "tensor_scalar", mneg[:, :], imp[:, :], mx2[:, 7:8], NEGM, ALU.is_lt, ALU.mult, reads=[t_i, t_d], writes=[t_i])
            S.op("pe", lambda e: e.transpose(ps[6][:, 0:128], mneg[:, :], cm["ident"][:]), reads=[t_i, cm["t_const"]], writes=[t_ps[6]])
            V(S, "act", "copy", mnegT[:, :].rearrange("p (g c) -> p g c", c=128), ps[6][:, 0:128].unsqueeze(1).to_broadcast([128, 3, 128]), reads=[t_ps[6]], writes=[t_mT])
            if DBG["level"] < 4:
                continue
            for kt in range(i + 1):
                AT.step(kslT[:, kt * 128:(kt + 1) * 128], tK, Q, tQ, (esel[:, kt * 128:(kt + 1) * 128], mnegT[:, :], t_mT),
                        caus[:, :] if kt == i else None, tc_, [(4, 129, vsl[:, kt, :], tK)], kt == 0)
            if DBG["level"] < 5:
                continue
            k0 = max(0, i - 4)
            for kt in range(k0, i + 1):
                mask = caus[:, :] if kt == i else (upper[:, :] if kt == i - 4 else None)
                AT.step(kwT[:, kt * 128:(kt + 1) * 128], tK, Q, tQ, None, mask, tc_, [(5, 129, vw[:, kt, :], tK)], kt == k0)
            if DBG["level"] < 6:
                continue
            for br, bank in enumerate((2, 4, 5)):
                V(S, "dve", "tensor_scalar", den[:, :].rearrange("p (g b) -> p g b", b=3)[:, :, br], ps[bank][:, 0:387].rearrange("p (g c) -> p g c", c=129)[:, :, 128], 1e-30, None, ALU.max,
                  reads=[t_ps[bank]], writes=[t_d])
            V(S, "dve", "reciprocal", den[:, :], den[:, :], reads=[t_d], writes=[t_d])
            V(S, "dve", "tensor_tensor", coef[:, :], den[:, :], gates[:, i, :], ALU.mult, reads=[t_d, t_gates], writes=[t_d])
            o, tO = ot[q], t_ot[q]
            for g in range(3):
                og = o[:, g * 128:(g + 1) * 128]
                V(S, "dve", "tensor_scalar", og, ps[2][:, g * 129:g * 129 + 128], coef[:, g * 3:g * 3 + 1], None, ALU.mult, reads=[t_ps[2], t_d], writes=[tO])
                V(S, "dve", "scalar_tensor_tensor", og, ps[4][:, g * 129:g * 129 + 128], coef[:, g * 3 + 1:g * 3 + 2], og, ALU.mult, ALU.add, reads=[t_ps[4], t_d, tO], writes=[tO])
                V(S, "dve", "scalar_tensor_tensor", ob[q][:, g * 128:(g + 1) * 128], ps[5][:, g * 129:g * 129 + 128], coef[:, g * 3 + 2:g * 3 + 3], og, ALU.mult, ALU.add, reads=[t_ps[5], t_d, tO], writes=[tO])
            DMA(S, "sp", out_d[i * 128:(i + 1) * 128, :], ob[q][:, :], f"ob{q}", reads=[tO], pwrites=[t_out])
        C.n = C2.n
    barrier(S)


def sub_ctx_unused(C, es):
    C2 = Ctx.__new__(Ctx)
    C2.nc, C2.es, C2.S, C2.n = C.nc, es, C.S, C.n + 1000
    return C2


def swa_proj(C, cm, xT_d, t_xT_d, pos_d, w_d, sc, D, NTOK):
    S = C.S
    with ExitStack() as es:
        C2 = sub_ctx(C, es)
        DC = D // 128
        XTt = [C2.sb(f"XTt{i}", [128, DC, 512], BF16) for i in range(2)]
        Wg = [C2.sb(f"Wg{i}", [128, DC * 512], BF16) for i in range(2)]
        P = Proj(C2, cm, D, XTt, [Tok(), Tok()], Wg, [Tok(), Tok()])
        rt = alloc_rope(C2)
        vst = [C2.sb(f"vst{i}", [128, 129], BF16) for i in range(2)]
        t_vst = [Tok(), Tok()]
        for i in range(2):
            V(S, "pool", "memset", vst[i][:, 128:129], 1.0, writes=[t_vst[i]])
        tiles = tiles_of(NTOK, 128)
        t_sc = sc["t"]
        for gi in range(3):
            Wv, tW = P.load_w(w_d, gi * 512, 512)
            for (r, T) in tiles[1:]:
                X, tX = P.load_x(xT_d, t_xT_d, r, T)
                rope_tables(C2, cm, rt, pos_d, r, T)
                for u in range(2):
                    qi0 = (r - 128) // 128
                    P.rope_pair(Wv, tW, u * 256, X, tX, T, rt, sc["QT"][:, qi0:qi0 + T // 128, gi * 2 + u, :], t_sc)
        Wv, tW = P.load_w(w_d, 1536, 384)
        vi = 0
        for (r, T) in tiles:
            X, tX = P.load_x(xT_d, t_xT_d, r, T)
            rope_tables(C2, cm, rt, pos_d, r, T)
            P.rope_pair(Wv, tW, 0, X, tX, T, rt, sc["kT"][:, r:r + T], t_sc)
            for ts in range(T // 128):
                bank = 4 + (vi % 2)
                k = vi % 2
                vi += 1
                A, tA = P.tm(Wv, tW, 256, 128, X, tX, ts, bank)
                V(S, "act", "copy", vst[k][:, 0:128], A, reads=[tA], pwrites=[t_vst[k]])
                DMA(S, "sp", sc["v"][:, r // 128 + ts, :], vst[k][:, :], f"vst{k}", reads=[t_vst[k]], pwrites=[t_sc])
        C.n = C2.n
    barrier(S)


def swa_attn(C, cm, sc, cst, sinks_d, mask0_d, out_d, t_out, NTOK):
    S = C.S
    NQ = (NTOK - 128) // 128
    ps, t_ps = cm["ps"], cm["t_ps"]
    with ExitStack() as es:
        C2 = sub_ctx(C, es)
        t_sc = sc["t"]
        tK = Tok()
        kT = C2.sb("kT", [128, NTOK], BF16)
        v = C2.sb("v", [128, NQ + 1, 129], BF16)
        DMA(S, "sp", kT[:, :], sc["kT"][:, :], "l0", reads=[t_sc], pwrites=[tK])
        DMA(S, "sp", v[:, :, :], sc["v"][:, :, :], "l1", reads=[t_sc], pwrites=[tK])
        tc_ = Tok()
        caus = C2.sb("caus", [128, 384], BF16)
        upper = C2.sb("upper", [128, 384], BF16)
        mask0 = C2.sb("mask0", [128, 384], BF16)
        load_cast(S, caus[:, :], cst["caus"][:, :], "c2", tc_)
        load_cast(S, upper[:, :], cst["upper"][:, :], "c3", tc_)
        load_cast(S, mask0[:, :], mask0_d[:, :], "c4", tc_)
        esink = C2.sb("esink", [128, 6], F32)
        DMA(S, "sp", esink[:, :], sinks_d[0:6].partition_broadcast(128), "c5", writes=[tc_])
        V(S, "act", "activation", esink[:, :], esink[:, :], AF.Exp, reads=[tc_], writes=[tc_])
        AT = Attn(C2, cm)
        Qt = [C2.sb(f"Qt{i}", [128, 384], BF16) for i in range(2)]
        t_Qt = [Tok(), Tok()]
        den = C2.sb("den", [128, 3], F32)
        t_d = Tok()
        ob = [C2.sb(f"ob{i}", [128, 768], BF16) for i in range(2)]
        t_ob = [Tok(), Tok()]
        qq = 0
        for i in range(NQ):
            oq = i % 2
            for hg in range(2):
                q = qq % 2
                qq += 1
                DMA(S, "sp", Qt[q][:, :], sc["QT"][:, i, 3 * hg:3 * hg + 3, :].rearrange("p a b -> p (a b)"), f"qt{q}", reads=[t_sc], writes=[t_Qt[q]])
                Q, tQ = Qt[q][:, :], t_Qt[q]
                AT.step(kT[:, i * 128:(i + 1) * 128], tK, Q, tQ, None, mask0[:, :] if i == 0 else upper[:, :], tc_, [(4, 129, v[:, i, :], tK)], True)
                AT.step(kT[:, (i + 1) * 128:(i + 2) * 128], tK, Q, tQ, None, caus[:, :], tc_, [(4, 129, v[:, i + 1, :], tK)], False)
                V(S, "dve", "tensor_tensor", den[:, :], ps[4][:, 0:387].rearrange("p (g c) -> p g c", c=129)[:, :, 128], esink[:, 3 * hg:3 * hg + 3], ALU.add, reads=[t_ps[4], tc_], writes=[t_d])
                V(S, "dve", "reciprocal", den[:, :], den[:, :], reads=[t_d], writes=[t_d])
                for g in range(3):
                    c0 = (3 * hg + g) * 128
                    V(S, "dve", "tensor_scalar", ob[oq][:, c0:c0 + 128], ps[4][:, g * 129:g * 129 + 128], den[:, g:g + 1], None, ALU.mult, reads=[t_ps[4], t_d], pwrites=[t_ob[oq]] if (hg, g) != (0, 0) else (),
                      writes=[t_ob[oq]] if (hg, g) == (0, 0) else ())
            DMA(S, "sp", out_d[i * 128:(i + 1) * 128, :], ob[oq][:, :], f"ob{oq}", reads=[t_ob[oq]], pwrites=[t_out])
        C.n = C2.n
    barrier(S)


def conv_phase(C, cm, xT_d, t_xT_d, w_d, dwT_d, dwb_d, lng_d, lnb_d, yc_d, out_d, t_out, D, NTOK):
    S = C.S
    NOWN = NTOK - 128
    ps, t_ps = cm["ps"], cm["t_ps"]
    with ExitStack() as es:
        C2 = sub_ctx(C, es)
        DC = D // 128
        XTt = [C2.sb(f"XTt{i}", [128, DC, 512], BF16) for i in range(2)]
        Wg = [C2.sb(f"Wg{i}", [128, DC * 256], BF16) for i in range(2)]
        P = Proj(C2, cm, D, XTt, [Tok(), Tok()], Wg, [Tok(), Tok()])
        u = [C2.sb(f"u{i}", [128, 30 + NOWN], F32) for i in range(2)]
        t_u = [Tok(), Tok()]
        y = [C2.sb(f"y{i}", [128, NOWN], F32) for i in range(2)]
        t_y = [Tok(), Tok()]
        sig = [C2.sb(f"sig{i}", [128, 512], F32) for i in range(2)]
        t_sig = [Tok(), Tok()]
        dw = C2.sb("dw", [128, 8, 31], F32)
        dwb = C2.sb("dwb", [128, 8], F32)
        tcw = Tok()
        DMA(S, "sp", dw[:, :, :], dwT_d.rearrange("(g p) j -> p g j", p=128), "cv0", pwrites=[tcw])
        S.op("sp", lambda e: e.dma_start(out=dwb[:, :], in_=dwb_d.rearrange("(g p) -> p g", p=128), allow_slow_non_contiguous=True), pwrites=[tcw], dma="cv1")
        yst = [C2.sb(f"yst{i}", [128, 512], F32) for i in range(2)]
        t_yst = [Tok(), Tok()]
        t_yc = Tok()
        tiles = tiles_of(NTOK, 128)
        si = 0
        yi = 0
        for cg in range(8):
            k = cg % 2
            eng = "dve"
            i = P.wi % 2
            P.wi += 1
            wv = w_d.rearrange("(c p) n -> p c n", p=128)
            Wv = Wg[i][:, 0:DC * 256].rearrange("p (c n) -> p c n", n=256)
            DMA(S, "pool", Wv[:, :, 0:128], wv[:, :, cg * 128:(cg + 1) * 128], f"wg{i}", writes=[P.t_Wg[i]])
            DMA(S, "pool", Wv[:, :, 128:256], wv[:, :, 1024 + cg * 128:1024 + (cg + 1) * 128], f"wg{i}", pwrites=[P.t_Wg[i]])
            tW = P.t_Wg[i]
            for (r, T) in tiles:
                X, tX = P.load_x(xT_d, t_xT_d, r, T)
                b = (P.pi % 2) * 2
                P.pi += 1
                A, tA = P.fm(Wv, tW, 0, X, tX, T, b)
                G, tG = P.fm(Wv, tW, 128, X, tX, T, b + 1)
                s = si % 2
                si += 1
                V(S, "act", "activation", sig[s][:, 0:T], G, AF.Sigmoid, reads=[tG], writes=[t_sig[s]])
                if r == 0:
                    V(S, "dve", "tensor_tensor", u[k][:, 0:30], A[:, 98:128], sig[s][:, 98:128], ALU.mult, reads=[tA, t_sig[s]], writes=[t_u[k]])
                else:
                    V(S, "dve", "tensor_tensor", u[k][:, 30 + r - 128:30 + r - 128 + T], A, sig[s][:, 0:T], ALU.mult, reads=[tA, t_sig[s]], pwrites=[t_u[k]])
            V(S, eng, "tensor_scalar", y[k][:, :], u[k][:, 0:NOWN], dw[:, cg, 0:1], dwb[:, cg:cg + 1], ALU.mult, ALU.add, reads=[t_u[k], tcw], writes=[t_y[k]])
            for j in range(1, 31):
                V(S, eng, "scalar_tensor_tensor", y[k][:, :], u[k][:, j:j + NOWN], dw[:, cg, j:j + 1], y[k][:, :], ALU.mult, ALU.add, reads=[t_u[k], tcw, t_y[k]], writes=[t_y[k]])
            for t4 in range(NOWN // 512):
                bank = 6 + (yi % 2)
                ys = yi % 2
                yi += 1
                for j in range(4):
                    tt_ = t4 * 4 + j
                    S.op("pe", (lambda bank=bank, j=j, k=k, tt_=tt_: lambda e: e.transpose(ps[bank][:, j * 128:(j + 1) * 128], y[k][:, tt_ * 128:(tt_ + 1) * 128], cm["ident"][:]))(),
                         reads=[t_y[k], cm["t_const"]], writes=[t_ps[bank]], inc=(j == 3))
                V(S, "act", "copy", yst[ys][:, :], ps[bank][:, 0:512], reads=[t_ps[bank]], writes=[t_yst[ys]])
                DMA(S, "sp", yc_d[t4 * 512:(t4 + 1) * 512, cg * 128:(cg + 1) * 128].rearrange("(a p) c -> p a c", p=128), yst[ys][:, :].rearrange("p (a c) -> p a c", c=128),
                    f"yst{ys}", reads=[t_yst[ys]], pwrites=[t_yc])
        G_ = C2.sb("lnG", [128, 1024], F32)
        B_ = C2.sb("lnB", [128, 1024], F32)
        DMA(S, "sp", G_[:, :], lng_d[0:1024].partition_broadcast(128), "cv2", pwrites=[tcw])
        DMA(S, "sp", B_[:, :], lnb_d[0:1024].partition_broadcast(128), "cv3", pwrites=[tcw])
        yr = [C2.sb(f"yr{i}", [128, 1024], F32) for i in range(2)]
        t_yr = [Tok(), Tok()]
        yo = [C2.sb(f"yo{i}", [128, 1024], BF16) for i in range(2)]
        t_yo = [Tok(), Tok()]
        tmp = cm["tmp"]
        ts_ = tmp["t_small"]
        for tt_ in range(NOWN // 128):
            k = tt_ % 2
            DMA(S, "sp", yr[k][:, :], yc_d[tt_ * 128:(tt_ + 1) * 128, :], f"yr{k}", reads=[t_yc], writes=[t_yr[k]])
            for h in range(2):
                V(S, "dve", "bn_stats", tmp["stats"][:, h, :], yr[k][:, h * 512:(h + 1) * 512], reads=[t_yr[k]], writes=[ts_])
            V(S, "dve", "bn_aggr", tmp["mv"][:, :], tmp["stats"][:, 0:2, :], reads=[ts_], writes=[ts_])
            V(S, "act", "activation", tmp["sd"][:, :], tmp["mv"][:, 1:2], AF.Sqrt, bias=tmp["eps"][:, 0:1], scale=1.0, reads=[ts_], writes=[ts_])
            V(S, "dve", "reciprocal", tmp["rstd"][:, :], tmp["sd"][:, :], reads=[ts_], writes=[ts_])
            V(S, "dve", "tensor_scalar", yr[k][:, :], yr[k][:, :], tmp["mv"][:, 0:1], tmp["rstd"][:, 0:1], ALU.subtract, ALU.mult, reads=[ts_, t_yr[k]], writes=[t_yr[k]])
            V(S, "pool", "tensor_tensor", yr[k][:, :], yr[k][:, :], G_[:, :], ALU.mult, reads=[t_yr[k], tcw], writes=[t_yr[k]])
            V(S, "pool", "tensor_tensor", yr[k][:, :], yr[k][:, :], B_[:, :], ALU.add, reads=[t_yr[k], tcw], writes=[t_yr[k]])
            V(S, "act", "activation", yo[k][:, :], yr[k][:, :], AF.Silu, reads=[t_yr[k]], writes=[t_yo[k]])
            DMA(S, "sp", out_d[tt_ * 128:(tt_ + 1) * 128, :], yo[k][:, :], f"yo{k}", reads=[t_yo[k]], pwrites=[t_out])
        C.n = C2.n
    barrier(S)

from concourse.bass_utils import run_bass_kernel_spmd

D_MODEL = 4096
D_FF = 11008
SEQ = 8192
NCORES = 8
NTC = 2048
ALPHA = 4 ** 0.25
EPS = 1e-5
_PROGS = {}


def _din(nc, name, shape, dt=F32):
    return nc.dram_tensor(name, list(shape), dt, kind="ExternalInput").ap()


def _dout(nc, name, shape, dt=F32):
    return nc.dram_tensor(name, list(shape), dt, kind="ExternalOutput").ap()


def _dint(nc, name, shape, dt=F32):
    return nc.dram_tensor(name, list(shape), dt, kind="Internal").ap()


def build_ffn_prog():
    nc = bass.Bass("TRN2", target_bir_lowering=False)
    x = _din(nc, "x", [NTC, D_MODEL])
    wgu = _din(nc, "wgu", [D_MODEL, 2 * D_FF])
    wd = _din(nc, "wd", [D_FF, D_MODEL])
    g = _din(nc, "g", [D_MODEL])
    b = _din(nc, "b", [D_MODEL])
    ident = _din(nc, "ident", [128, 128])
    z = _dint(nc, "z", [NTC, D_MODEL])
    y = _dout(nc, "y", [NTC, D_MODEL])
    with ExitStack() as es:
        C = Ctx(nc, es)
        cm = alloc_common(C, D_MODEL)
        init_common(C, cm, ident)
        ffn_stage(C, cm, x, wgu, wd, g, b, z, y, NTC, D_MODEL, D_FF, ALPHA, EPS)
        C.S.emit()
    return nc


def build_c_prog():
    nc = bass.Bass("TRN2", target_bir_lowering=False)
    mixT = _din(nc, "mixT", [D_MODEL, NTC], BF16)
    x = _din(nc, "x", [NTC, D_MODEL])
    wo = _din(nc, "wo", [D_MODEL, D_MODEL])
    g2 = _din(nc, "g2", [D_MODEL])
    b2 = _din(nc, "b2", [D_MODEL])
    wgu = _din(nc, "wgu", [D_MODEL, 2 * D_FF])
    wd = _din(nc, "wd", [D_FF, D_MODEL])
    g = _din(nc, "g", [D_MODEL])
    b = _din(nc, "b", [D_MODEL])
    ident = _din(nc, "ident", [128, 128])
    z = _dint(nc, "z", [NTC, D_MODEL])
    x2 = _dint(nc, "x2", [NTC, D_MODEL])
    y = _dout(nc, "y", [NTC, D_MODEL])
    with ExitStack() as es:
        C = Ctx(nc, es)
        cm = alloc_common(C, D_MODEL)
        init_common(C, cm, ident)
        with ExitStack() as e1:
            C1 = sub_ctx(C, e1)
            outproj_stage(C1, cm, mixT, x, wo, g2, b2, z, x2, NTC, D_MODEL, ALPHA, EPS)
            C.n = C1.n
        barrier(C.S)
        with ExitStack() as e2:
            C2 = sub_ctx(C, e2)
            ffn_stage(C2, cm, x2, wgu, wd, g, b, z, y, NTC, D_MODEL, D_FF, ALPHA, EPS)
            C.n = C2.n
        C.S.emit()
    return nc


N_SWA = 4096 + 128
N_CONV = 2048 + 128


def build_mixer_prog(parts=("nsa", "swa", "conv")):
    nc = bass.Bass("TRN2", target_bir_lowering=False)
    D = D_MODEL
    x_nsa = _din(nc, "x_nsa", [SEQ, D])
    pos_nsa = _din(nc, "pos_nsa", [SEQ], I32)
    w_nsa = _din(nc, "w_nsa", [D, 1936])
    gateb = _din(nc, "gateb", [9])
    cmp_pos = _din(nc, "cmp_pos", [2, 32, 128])
    cmp_w1 = _din(nc, "cmp_w1", [2, 4096, 128])
    cmp_w2 = _din(nc, "cmp_w2", [2, 128, 128])
    x_swa = _din(nc, "x_swa", [N_SWA, D])
    pos_swa = _din(nc, "pos_swa", [N_SWA], I32)
    w_swa = _din(nc, "w_swa", [D, 1920])
    sinks = _din(nc, "sinks", [6])
    mask0 = _din(nc, "mask0", [128, 384])
    x_conv = _din(nc, "x_conv", [N_CONV, D])
    w_conv = _din(nc, "w_conv", [D, 2048])
    dwT = _din(nc, "dwT", [1024, 31])
    dwb = _din(nc, "dwb", [1024])
    clng = _din(nc, "clng", [1024])
    clnb = _din(nc, "clnb", [1024])
    ident = _din(nc, "ident", [128, 128])
    inv_d = _din(nc, "inv", [128, 1])
    sgn_d = _din(nc, "sgn", [128, 1])
    cst = {
        "esel": _din(nc, "esel", [128, SEQ]), "cmask": _din(nc, "cmask", [128, 17, 384]), "caus": _din(nc, "caus", [128, 384]),
        "upper": _din(nc, "upper", [128, 384]), "ovl": _din(nc, "ovl", [128, 4, 128]), "fpat": _din(nc, "fpat", [128, 3]),
    }
    nsa_out = _dout(nc, "nsa_out", [SEQ, 384], BF16)
    swa_out = _dout(nc, "swa_out", [4096, 768], BF16)
    conv_out = _dout(nc, "conv_out", [2048, 1024], BF16)
    xT_d = _dint(nc, "xT_d", [128, 32, SEQ], BF16)
    NQ = SEQ // 128
    sc = {"QT": _dint(nc, "s_QT", [128, NQ, 3, 128], BF16), "t": Tok()}
    for n in ("kcT", "kslT", "kwT", "vcT"):
        sc[n] = _dint(nc, "s_" + n, [128, SEQ], BF16)
    sc["vsl"] = _dint(nc, "s_vsl", [128, NQ, 129], BF16)
    sc["vw"] = _dint(nc, "s_vw", [128, NQ, 129], BF16)
    sc2 = {"QT": _dint(nc, "s2_QT", [128, 32, 6, 128], BF16), "kT": _dint(nc, "s2_kT", [128, N_SWA], BF16), "v": _dint(nc, "s2_v", [128, 33, 129], BF16), "t": Tok()}
    yc_d = _dint(nc, "yc_d", [2048, 1024])
    with ExitStack() as es:
        C = Ctx(nc, es)
        S = C.S
        cm = alloc_common(C, D)
        init_common(C, cm, ident)
        cm["inv"] = C.sb("inv", [128, 1], F32)
        cm["sgn"] = C.sb("sgn", [128, 1], F32)
        DMA(S, "sp", cm["inv"][:, :], inv_d[:, :], "const", pwrites=[cm["t_const"]])
        DMA(S, "sp", cm["sgn"][:, :], sgn_d[:, :], "const", pwrites=[cm["t_const"]])
        gates = C.sb("gates", [128, NQ, 9], F32)
        t_gates = Tok()
        t_out = Tok()

        def do_xT(x_d, ntok):
            t_xT = Tok()
            with ExitStack() as e1:
                C1 = sub_ctx(C, e1)
                XTt = [C1.sb(f"XTt{i}", [128, 32, 512], BF16) for i in range(2)]
                xT_stage(C1, cm, x_d, ntok, xT_d, t_xT, D, XTt, [Tok(), Tok()])
                C.n = C1.n
            barrier(S)
            return t_xT

        if "nsa" in parts:
            t_xT = do_xT(x_nsa, SEQ)
            nsa_proj(C, cm, xT_d, t_xT, pos_nsa, w_nsa, gateb, sc, gates, t_gates, D, SEQ)
            nsa_attn(C, cm, sc, gates, t_gates, cst, cmp_pos, cmp_w1, cmp_w2, nsa_out, t_out, SEQ)
        if "swa" in parts:
            t_xT = do_xT(x_swa, N_SWA)
            swa_proj(C, cm, xT_d, t_xT, pos_swa, w_swa, sc2, D, N_SWA)
            swa_attn(C, cm, sc2, cst, sinks, mask0, swa_out, t_out, N_SWA)
        if "conv" in parts:
            t_xT = do_xT(x_conv, N_CONV)
            conv_phase(C, cm, xT_d, t_xT, w_conv, dwT, dwb, clng, clnb, yc_d, conv_out, t_out, D, N_CONV)
        print("mixer ops", S.nops, {k: len(v) for k, v in S.q.items()}, "sems", len(S.sem))
        S.emit()
    return nc


def _consts():
    c = {}
    c["ident"] = np.eye(128, dtype=np.float32)
    d = np.arange(128) % 64
    c["inv"] = np.power(np.float32(10000.0), -(2.0 * d).astype(np.float32) / np.float32(128.0)).astype(np.float32).reshape(128, 1)
    c["sgn"] = np.where(np.arange(128) < 64, -1.0, 1.0).astype(np.float32).reshape(128, 1)
    k = np.arange(SEQ)
    c["esel"] = (k[None, :] // 64 == np.arange(128)[:, None]).astype(np.float32)
    nn = np.arange(128)[:, None, None]
    dl = np.arange(17)[None, :, None]
    qq = (np.arange(384) % 128)[None, None, :]
    c["cmask"] = (16 * nn + 31 - 128 * dl <= qq).astype(np.float32)
    kk = np.arange(128)[:, None]
    q2 = (np.arange(384) % 128)[None, :]
    c["caus"] = (kk <= q2).astype(np.float32)
    c["upper"] = (kk > q2).astype(np.float32)
    n_cmp = 511
    cs = np.arange(512) * 16
    ss = np.arange(128) * 64
    ov = ((cs[:, None] < ss[None, :] + 64) & (cs[:, None] + 32 > ss[None, :])).astype(np.float32)
    ov[n_cmp:] = 0
    c["ovl"] = np.ascontiguousarray(ov.reshape(4, 128, 128).transpose(1, 0, 2))
    lowq = (np.arange(128) < 64)
    fp = np.zeros((128, 3), np.float32)
    fp[:, 0] = np.where(lowq, 1e4, 0.0)
    fp[:, 1] = np.where(lowq, 2e4, 1e4)
    fp[:, 2] = np.where(lowq, 0.0, 2e4)
    c["fpat"] = fp
    return c


def _swap(c0):
    return list(range(c0 + 64, c0 + 128)) + list(range(c0, c0 + 64))


def _rng(c0, n=128):
    return list(range(c0, c0 + n))


def mixer_inputs(x1, positions, p, l, cst):
    w_in = p["w_in"][l]
    maps = []
    for c in range(NCORES):
        b, h = c // 4, c % 4
        m = dict(cst)
        cols = []
        for g in range(3):
            c0 = 2048 + (3 * h + g) * 128
            cols += _rng(c0) + _swap(c0)
        for base in (3584, 4608, 5632):
            cols += _rng(base + h * 128) + _swap(base + h * 128)
        cols += _rng(4096 + h * 128)
        cols += _rng(5120 + h * 128) + _rng(6144 + h * 128) + _rng(6656 + 9 * h, 9)
        wn = np.zeros((D_MODEL, 1936), np.float32)
        wn[:, 0:1929] = w_in[:, cols]
        m["w_nsa"] = wn
        m["x_nsa"] = np.ascontiguousarray(x1[b])
        m["pos_nsa"] = np.ascontiguousarray(positions[b]).astype(np.int32)
        m["gateb"] = np.ascontiguousarray(p["nsa_gate_b"][l][9 * h:9 * h + 9])
        m["cmp_pos"] = np.ascontiguousarray(p["nsa_cmp_pos"][l])
        m["cmp_w1"] = np.ascontiguousarray(p["nsa_cmp_w1"][l])
        m["cmp_w2"] = np.ascontiguousarray(p["nsa_cmp_w2"][l])
        kvh, half = h // 2, h % 2
        cols = []
        for j in range(6):
            c0 = 6692 + (kvh * 6 + j) * 128
            cols += _rng(c0) + _swap(c0)
        cols += _rng(8228 + kvh * 128) + _swap(8228 + kvh * 128) + _rng(8484 + kvh * 128)
        m["w_swa"] = np.ascontiguousarray(w_in[:, cols])
        xs = np.zeros((N_SWA, D_MODEL), np.float32)
        ps_ = np.zeros((N_SWA,), np.int32)
        t0 = half * 4096
        if half > 0:
            xs[:] = x1[b, t0 - 128:t0 + 4096]
            ps_[:] = positions[b, t0 - 128:t0 + 4096]
        else:
            xs[128:] = x1[b, 0:4096]
            ps_[128:] = positions[b, 0:4096]
        m["x_swa"], m["pos_swa"] = xs, ps_
        m["sinks"] = np.ascontiguousarray(p["swa_sinks"][l][kvh * 6:kvh * 6 + 6])
        m["mask0"] = cst["upper"] if half > 0 else np.zeros_like(cst["upper"])
        xc = np.zeros((N_CONV, D_MODEL), np.float32)
        t0 = h * 2048
        if h > 0:
            xc[:] = x1[b, t0 - 128:t0 + 2048]
        else:
            xc[128:] = x1[b, 0:2048]
        m["x_conv"] = xc
        m["w_conv"] = np.ascontiguousarray(w_in[:, 0:2048])
        m["dwT"] = np.ascontiguousarray(p["conv_dw_w"][l].T)
        m["dwb"] = np.ascontiguousarray(p["conv_dw_b"][l])
        m["clng"] = np.ascontiguousarray(p["conv_ln_g"][l])
        m["clnb"] = np.ascontiguousarray(p["conv_ln_b"][l])
        maps.append(m)
    return maps


def assemble_mix(res):
    bf = res[0]["nsa_out"].dtype
    mix = np.zeros((2, SEQ, D_MODEL), dtype=bf)
    for c in range(NCORES):
        b, h = c // 4, c % 4
        mix[b, h * 2048:(h + 1) * 2048, 0:1024] = res[c]["conv_out"]
        mix[b, :, 1024 + h * 384:1024 + (h + 1) * 384] = res[c]["nsa_out"]
        kvh, half = h // 2, h % 2
        mix[b, half * 4096:(half + 1) * 4096, 2560 + kvh * 768:2560 + (kvh + 1) * 768] = res[c]["swa_out"]
    flat = mix.reshape(2 * SEQ, D_MODEL)
    return [np.ascontiguousarray(flat[c * NTC:(c + 1) * NTC].T) for c in range(NCORES)], mix


def _prog(name, fn):
    if name not in _PROGS:
        _PROGS[name] = fn()
    return _PROGS[name]


def _run(nc, maps):
    res = run_bass_kernel_spmd(nc, maps, core_ids=list(range(NCORES)))
    return res.results


def kernel(**inputs):
    p = {k: np.asarray(v) for k, v in inputs.items()}
    x = np.ascontiguousarray(p["x"], dtype=np.float32).reshape(2 * SEQ, D_MODEL)
    positions = p["positions"]
    cst = _consts()
    ident = cst["ident"]
    for l in range(2):
        nc = _prog("ffn", build_ffn_prog)
        maps = [{"x": np.ascontiguousarray(x[c * NTC:(c + 1) * NTC]), "wgu": p["ffn1_w_gu"][l], "wd": p["ffn1_w_down"][l], "g": p["ln1_g"][l], "b": p["ln1_b"][l], "ident": ident} for c in range(NCORES)]
        res = _run(nc, maps)
        x1 = np.concatenate([r["y"] for r in res], axis=0)
        nc = _prog("mixer", build_mixer_prog)
        res = _run(nc, mixer_inputs(x1.reshape(2, SEQ, D_MODEL), positions, p, l, cst))
        mixT, _ = assemble_mix(res)
        nc = _prog("c", build_c_prog)
        maps = [{"mixT": mixT[c], "x": np.ascontiguousarray(x1[c * NTC:(c + 1) * NTC]), "wo": p["w_out"][l], "g2": p["ln2_g"][l], "b2": p["ln2_b"][l],
                 "wgu": p["ffn2_w_gu"][l], "wd": p["ffn2_w_down"][l], "g": p["ln3_g"][l], "b": p["ln3_b"][l], "ident": ident} for c in range(NCORES)]
        res = _run(nc, maps)
        x = np.concatenate([r["y"] for r in res], axis=0)
    return x.reshape(2, SEQ, D_MODEL).astype(np.float32)
```
